# Optimizing a Trainium2 kernel written in Bass

```python
import math
import jax, jax.numpy as jnp
from jax import lax
import numpy as np

D_MODEL = 2048
BATCH = 4
SEQ = 4096
DEPTH = 4

N_A = DEPTH // 2
N_B = DEPTH - N_A
SSM_WIDTH = D_MODEL
SSM_GROUP = 16
SSM_GROUPS = SSM_WIDTH // SSM_GROUP
SSM_STATE = 64
N_HEADS = 16
N_KV = 4
HPG = N_HEADS // N_KV
HEAD_DIM = D_MODEL // N_HEADS
ATT_WIDTH = N_HEADS * HEAD_DIM
N_BRANCH = 3
CMP_LEN = 32
CMP_STRIDE = 16
SEL_LEN = 64
SEL_TOPK = 16
WINDOW = 512
WIN_QBLOCK = 128
SEL_QCHUNK = 32
SEL_BONUS = 1e3
NEG = -1e30
EPS = 1e-6

kernel_name = "yoco_s5_nsa_hybrid"


def rmsnorm(x, g):
    xf = x.astype(jnp.float32)
    y = xf * lax.rsqrt(jnp.mean(xf * xf, axis=-1, keepdims=True) + EPS)
    return (y * g.astype(jnp.float32)).astype(x.dtype)


def modulate(h, shift, scale):
    return h * (1.0 + scale[:, None, :]) + shift[:, None, :]


def masked_softmax(s, mask):
    s = jnp.where(mask, s.astype(jnp.float32), NEG)
    p = jax.nn.softmax(s, axis=-1)
    return jnp.where(mask, p, 0.0)


def s5_discretize(lam_re, lam_im, log_step, b_re, b_im):
    dt = jnp.exp(log_step.astype(jnp.float32))[:, None]
    lr, li = lam_re.astype(jnp.float32), lam_im.astype(jnp.float32)
    mag = jnp.exp(lr * dt)
    a_re, a_im = mag * jnp.cos(li * dt), mag * jnp.sin(li * dt)
    den = lr * lr + li * li
    coef_re = ((a_re - 1.0) * lr + a_im * li) / den
    coef_im = (a_im * lr - (a_re - 1.0) * li) / den
    br, bi = b_re.astype(jnp.float32), b_im.astype(jnp.float32)
    bb_re = coef_re[..., None] * br - coef_im[..., None] * bi
    bb_im = coef_re[..., None] * bi + coef_im[..., None] * br
    return a_re, a_im, bb_re, bb_im


def _ssm_combine(left, right):
    ar_i, ai_i, br_i, bi_i = left
    ar_j, ai_j, br_j, bi_j = right
    return (ar_j * ar_i - ai_j * ai_i,
            ar_j * ai_i + ai_j * ar_i,
            ar_j * br_i - ai_j * bi_i + br_j,
            ar_j * bi_i + ai_j * br_i + bi_j)


def _scan_one(a_re, a_im, bu_re, bu_im):
    out = lax.associative_scan(_ssm_combine, (a_re, a_im, bu_re, bu_im), axis=0)
    return out[2], out[3]


def s5_mixer(h, w_in, lam_re, lam_im, log_step, b_re, b_im, c_re, c_im, d_skip, w_glu, b_glu, w_out):
    B_, L, _ = h.shape
    u, z = jnp.split(h @ w_in, 2, axis=-1)
    ug = u.reshape(B_, L, SSM_GROUPS, SSM_GROUP).astype(jnp.float32)
    a_re, a_im, bb_re, bb_im = s5_discretize(lam_re, lam_im, log_step, b_re, b_im)
    bu_re = jnp.einsum('blgc,gnc->blgn', ug, bb_re)
    bu_im = jnp.einsum('blgc,gnc->blgn', ug, bb_im)
    shp = (L, SSM_GROUPS, SSM_STATE)
    x_re, x_im = jax.vmap(_scan_one, in_axes=(None, None, 0, 0))(
        jnp.broadcast_to(a_re, shp), jnp.broadcast_to(a_im, shp), bu_re, bu_im)
    y = (jnp.einsum('blgn,gcn->blgc', x_re, c_re.astype(jnp.float32))
         - jnp.einsum('blgn,gcn->blgc', x_im, c_im.astype(jnp.float32)))
    y = y + d_skip.astype(jnp.float32).reshape(SSM_GROUPS, SSM_GROUP) * ug
    y = jax.nn.gelu(y.reshape(B_, L, SSM_WIDTH).astype(h.dtype))
    y = y * jax.nn.sigmoid(y @ w_glu + b_glu)
    return (y * jax.nn.silu(z)) @ w_out


def compress_blocks(t, blk_idx, pe, w1, b1, w2, b2):
    blocks = t[:, blk_idx] + pe[None, None, :, None, :]
    B_, n = blocks.shape[:2]
    flat = jnp.moveaxis(blocks, 3, 2).reshape(B_, n, N_KV, CMP_LEN * HEAD_DIM)
    return jax.nn.gelu(flat @ w1 + b1) @ w2 + b2


def nsa_shared_kv(h_kv, w_kv, cmp_pe, cmp_w1, cmp_b1, cmp_w2, cmp_b2):
    B_, L, _ = h_kv.shape
    kv = (h_kv @ w_kv).reshape(B_, L, 2 * N_BRANCH, N_KV, HEAD_DIM)
    n_cmp = (L - CMP_LEN) // CMP_STRIDE + 1
    blk_idx = np.arange(n_cmp)[:, None] * CMP_STRIDE + np.arange(CMP_LEN)[None, :]
    kc = compress_blocks(kv[:, :, 0], blk_idx, cmp_pe[0], cmp_w1[0], cmp_b1[0], cmp_w2[0], cmp_b2[0])
    vc = compress_blocks(kv[:, :, 1], blk_idx, cmp_pe[1], cmp_w1[1], cmp_b1[1], cmp_w2[1], cmp_b2[1])
    return kc, vc, kv[:, :, 2], kv[:, :, 3], kv[:, :, 4], kv[:, :, 5]


def cmp_attention(q, kc, vc, pos):
    n_cmp = kc.shape[1]
    s = jnp.einsum('blghd,bngd->bghln', q, kc)
    blk_end = jnp.arange(n_cmp) * CMP_STRIDE + CMP_LEN - 1
    mask = blk_end[None, :] <= pos[:, None]
    p = masked_softmax(s, mask)
    o = jnp.einsum('bghln,bngd->blghd', p.astype(vc.dtype), vc)
    return o, p.sum(axis=2)


def selection_overlap(n_cmp, n_slc):
    c0 = np.arange(n_cmp)[:, None] * CMP_STRIDE
    s0 = np.arange(n_slc)[None, :] * SEL_LEN
    ov = np.clip(np.minimum(c0 + CMP_LEN, s0 + SEL_LEN) - np.maximum(c0, s0), 0, None)
    return (ov / CMP_STRIDE).astype(np.float32)


def select_blocks(p_cmp, pos):
    n_cmp = p_cmp.shape[-1]
    n_slc = pos.shape[0] // SEL_LEN
    p_slc = jnp.einsum('bgln,ns->bgls', p_cmp, jnp.asarray(selection_overlap(n_cmp, n_slc)))
    blk = jnp.arange(n_slc)[None, :]
    cur = (pos // SEL_LEN)[:, None]
    valid = blk <= cur
    forced = ((blk == 0) | (blk == cur) | (blk == cur - 1)).astype(jnp.float32)
    score = jnp.where(valid, p_slc + SEL_BONUS * forced, -SEL_BONUS)
    top, idx = lax.top_k(score, min(SEL_TOPK, n_slc))
    return idx, top > -0.5 * SEL_BONUS


def selected_attention(q, ks, vs, idx, ok):
    B_, L = q.shape[:2]
    n_slc = L // SEL_LEN
    n_ch = L // SEL_QCHUNK
    kb = jnp.moveaxis(ks.reshape(B_, n_slc, SEL_LEN, N_KV, HEAD_DIM), 3, 1)
    vb = jnp.moveaxis(vs.reshape(B_, n_slc, SEL_LEN, N_KV, HEAD_DIM), 3, 1)
    q_ch = jnp.moveaxis(q.reshape(B_, n_ch, SEL_QCHUNK, N_KV, HPG, HEAD_DIM), 1, 0)
    idx_ch = jnp.moveaxis(idx.reshape(B_, N_KV, n_ch, SEL_QCHUNK, -1), 2, 0)
    ok_ch = jnp.moveaxis(ok.reshape(B_, N_KV, n_ch, SEL_QCHUNK, -1), 2, 0)
    gather = jax.vmap(jax.vmap(lambda blocks, ix: blocks[ix]))
    offs = jnp.arange(SEL_LEN)

    def chunk(args):
        ci, qc, ic, oc = args
        t = ci * SEL_QCHUNK + jnp.arange(SEL_QCHUNK)
        kg = gather(kb, ic)
        vg = gather(vb, ic)
        s = jnp.einsum('bqghd,bgqksd->bgqhks', qc, kg)
        kpos = ic[..., None] * SEL_LEN + offs
        mask = (oc[..., None] & (kpos <= t[None, None, :, None, None]))[:, :, :, None]
        p = masked_softmax(s.reshape(*s.shape[:4], -1), mask.reshape(*mask.shape[:4], -1))
        p = p.reshape(s.shape).astype(vg.dtype)
        return jnp.einsum('bgqhks,bgqksd->bqghd', p, vg)

    o = lax.map(chunk, (jnp.arange(n_ch), q_ch, idx_ch, ok_ch))
    return jnp.moveaxis(o, 0, 1).reshape(B_, L, N_KV, HPG, HEAD_DIM)


def window_attention(q, kw, vw):
    B_, L = q.shape[:2]
    nb = L // WIN_QBLOCK
    span = WIN_QBLOCK + WINDOW
    pad = ((0, 0), (WINDOW, 0), (0, 0), (0, 0))
    kp, vp = jnp.pad(kw, pad), jnp.pad(vw, pad)
    q_blk = jnp.moveaxis(q.reshape(B_, nb, WIN_QBLOCK, N_KV, HPG, HEAD_DIM), 1, 0)
    koff = np.arange(span) - WINDOW
    rel = np.arange(WIN_QBLOCK)[:, None] - koff[None, :]
    band = (rel >= 0) & (rel < WINDOW)

    def block(args):
        bi, qb = args
        start = bi * WIN_QBLOCK
        kb = lax.dynamic_slice_in_dim(kp, start, span, axis=1)
        vb = lax.dynamic_slice_in_dim(vp, start, span, axis=1)
        mask = band & ((start + koff) >= 0)[None, :]
        s = jnp.einsum('bqghd,bkgd->bghqk', qb, kb)
        p = masked_softmax(s, mask).astype(vb.dtype)
        return jnp.einsum('bghqk,bkgd->bqghd', p, vb)

    o = lax.map(block, (jnp.arange(nb), q_blk))
    return jnp.moveaxis(o, 0, 1).reshape(B_, L, N_KV, HPG, HEAD_DIM)


def nsa_mixer(h, w_qg, w_o, kc, vc, ks, vs, kw, vw):
    B_, L, _ = h.shape
    proj = h @ w_qg
    q = proj[..., :ATT_WIDTH].reshape(B_, L, N_KV, HPG, HEAD_DIM) * (HEAD_DIM ** -0.5)
    g_end = ATT_WIDTH + N_BRANCH * N_HEADS
    gates = jax.nn.sigmoid(proj[..., ATT_WIDTH:g_end].astype(jnp.float32)).astype(h.dtype)
    gates = gates.reshape(B_, L, N_BRANCH, N_KV, HPG, 1)
    z = proj[..., g_end:].reshape(B_, L, N_BRANCH, N_KV, HPG, HEAD_DIM)
    pos = jnp.arange(L)
    o_cmp, p_cmp = cmp_attention(q, kc, vc, pos)
    idx, ok = select_blocks(p_cmp, pos)
    o_sel = selected_attention(q, ks, vs, idx, ok)
    o_win = window_attention(q, kw, vw)
    o_br = jnp.stack([o_cmp, o_sel, o_win], axis=2)
    o = jnp.sum(gates * jax.nn.silu(z) * o_br, axis=2)
    return o.reshape(B_, L, ATT_WIDTH) @ w_o


def setup_inputs(seed: int = 0) -> dict:
    key = jax.random.key(seed)
    keys = list(jax.random.split(key, 40))

    def nrm(shape, s):
        return s * jax.random.normal(keys.pop(), shape, jnp.float32)

    D, E, G, N = D_MODEL, SSM_WIDTH, SSM_GROUPS, SSM_STATE
    qg_cols = ATT_WIDTH + N_BRANCH * N_HEADS + N_BRANCH * ATT_WIDTH
    lam_im0 = jnp.pi * jnp.arange(N, dtype=jnp.float32)
    return {
        "x": nrm((BATCH, SEQ, D), 1.0),
        "c": nrm((BATCH, D), 1.0),
        "norm_g": 1.0 + nrm((DEPTH, D), 0.02),
        "mod_w": nrm((DEPTH, D, 3 * D), 0.5 * D ** -0.5),
        "mod_b": nrm((DEPTH, 3 * D), 0.01),
        "ssm_w_in": nrm((N_A, D, 2 * E), D ** -0.5),
        "ssm_lam_re": -0.5 + nrm((N_A, G, N), 0.01),
        "ssm_lam_im": lam_im0 + nrm((N_A, G, N), 0.01),
        "ssm_log_step": jax.random.uniform(keys.pop(), (N_A, G), jnp.float32, math.log(1e-3), math.log(1e-1)),
        "ssm_b_re": nrm((N_A, G, N, SSM_GROUP), (2 * SSM_GROUP) ** -0.5),
        "ssm_b_im": nrm((N_A, G, N, SSM_GROUP), (2 * SSM_GROUP) ** -0.5),
        "ssm_c_re": nrm((N_A, G, SSM_GROUP, N), 0.5),
        "ssm_c_im": nrm((N_A, G, SSM_GROUP, N), 0.5),
        "ssm_d": nrm((N_A, E), 1.0),
        "ssm_w_glu": nrm((N_A, E, E), E ** -0.5),
        "ssm_b_glu": nrm((N_A, E), 0.01),
        "ssm_w_out": nrm((N_A, E, D), E ** -0.5),
        "kv_norm_g": 1.0 + nrm((D,), 0.02),
        "kv_mod_w": nrm((D, 2 * D), 0.5 * D ** -0.5),
        "kv_mod_b": nrm((2 * D,), 0.01),
        "w_kv": nrm((D, 2 * N_BRANCH * N_KV * HEAD_DIM), D ** -0.5),
        "cmp_pe": nrm((2, CMP_LEN, HEAD_DIM), 0.02),
        "cmp_w1": nrm((2, CMP_LEN * HEAD_DIM, HEAD_DIM), (CMP_LEN * HEAD_DIM) ** -0.5),
        "cmp_b1": nrm((2, HEAD_DIM), 0.01),
        "cmp_w2": nrm((2, HEAD_DIM, HEAD_DIM), HEAD_DIM ** -0.5),
        "cmp_b2": nrm((2, HEAD_DIM), 0.01),
        "nsa_w_qg": nrm((N_B, D, qg_cols), D ** -0.5),
        "nsa_w_o": nrm((N_B, ATT_WIDTH, D), ATT_WIDTH ** -0.5),
        "final_norm_g": 1.0 + nrm((D,), 0.02),
    }


def reference(x, c, norm_g, mod_w, mod_b, ssm_w_in, ssm_lam_re, ssm_lam_im, ssm_log_step,
              ssm_b_re, ssm_b_im, ssm_c_re, ssm_c_im, ssm_d, ssm_w_glu, ssm_b_glu, ssm_w_out,
              kv_norm_g, kv_mod_w, kv_mod_b, w_kv, cmp_pe, cmp_w1, cmp_b1, cmp_w2, cmp_b2,
              nsa_w_qg, nsa_w_o, final_norm_g):
    c_act = jax.nn.silu(c)
    shared = None
    for layer in range(DEPTH):
        if layer == N_A:
            kv_shift, kv_scale = jnp.split(c_act @ kv_mod_w + kv_mod_b, 2, axis=-1)
            h_kv = modulate(rmsnorm(x, kv_norm_g), kv_shift, kv_scale)
            shared = nsa_shared_kv(h_kv, w_kv, cmp_pe, cmp_w1, cmp_b1, cmp_w2, cmp_b2)
        shift, scale, gate = jnp.split(c_act @ mod_w[layer] + mod_b[layer], 3, axis=-1)
        h = modulate(rmsnorm(x, norm_g[layer]), shift, scale)
        if layer < N_A:
            out = s5_mixer(h, ssm_w_in[layer], ssm_lam_re[layer], ssm_lam_im[layer],
                           ssm_log_step[layer], ssm_b_re[layer], ssm_b_im[layer],
                           ssm_c_re[layer], ssm_c_im[layer], ssm_d[layer],
                           ssm_w_glu[layer], ssm_b_glu[layer], ssm_w_out[layer])
        else:
            j = layer - N_A
            out = nsa_mixer(h, nsa_w_qg[j], nsa_w_o[j], *shared)
        x = x + gate[:, None, :] * out
    return rmsnorm(x, final_norm_g)
```

```python
import numpy as np
import concourse.bass as bass
import concourse.mybir as mybir

F32 = mybir.dt.float32
BF16 = mybir.dt.bfloat16
I32 = mybir.dt.int32
AF = mybir.ActivationFunctionType
ALU = mybir.AluOpType
AX = mybir.AxisListType

EPOCH = 20000
NDMA_SEM = 10


class Prog:
    ENGS = ("pe", "dve", "act", "pool", "sp")

    def __init__(self, nc):
        self.nc = nc
        self.ops = []
        self.ctx = []

    def op(self, eng, fn, reads=(), writes=(), dma=False):
        self.ops.append((eng, fn, tuple(reads), tuple(writes), dma))

    def dma(self, q, out, in_, reads=(), writes=(), **kw):
        self.op(q, lambda e: e.dma_start(out=out, in_=in_, **kw), reads, writes, dma=True)

    def emit(self, stack):
        nc = self.nc
        SS = SemState.get(nc)
        esems, dsems, dcount, drr, ccount, known, semobj = SS.esems, SS.dsems, SS.dcount, SS.drr, SS.ccount, SS.known, SS.semobj
        last_w = {}
        readers = {}
        streams = {e: [] for e in self.ENGS}

        def need(eng, tok, waits):
            if tok is None:
                return
            sid, val = tok
            if eng == "pe" and sid[0] == "c" and sid[1] == "pe":
                return
            if known[eng].get(sid, 0) >= val:
                return
            known[eng][sid] = val
            waits[sid] = max(waits.get(sid, 0), val)

        for (eng, fn, r, w, dma) in self.ops:
            waits = {}
            for k in r:
                need(eng, last_w.get(k), waits)
            for k in w:
                need(eng, last_w.get(k), waits)
                for t in readers.get(k, ()):
                    need(eng, t, waits)
            if dma:
                i = drr[eng]
                drr[eng] = (i + 1) % NDMA_SEM
                s = dsems[eng][i]
                sid = ("d", eng, i)
                semobj[sid] = s
                if dcount[eng][i] > 0:
                    need(eng, (sid, dcount[eng][i]), waits)
                dcount[eng][i] += 16
                tok = (sid, dcount[eng][i])
                inc = (s, 16)
            else:
                c = ccount[eng]
                ep = c // EPOCH
                s = esems[eng][ep]
                sid = ("c", eng, ep)
                semobj[sid] = s
                ccount[eng] = c + 1
                tok = (sid, (c % EPOCH) + 1)
                inc = (s, 1)
            streams[eng].append((waits, fn, inc))
            for k in w:
                last_w[k] = tok
                readers[k] = []
            for k in r:
                readers.setdefault(k, []).append(tok)

        final_waits = {}
        for e in dsems:
            for i in range(NDMA_SEM):
                if dcount[e][i] > 0:
                    final_waits[("d", e, i)] = dcount[e][i]
        for e in self.ENGS:
            c = ccount[e]
            if c > 0:
                ep = (c - 1) // EPOCH
                final_waits[("c", e, ep)] = ((c - 1) % EPOCH) + 1
        for e in self.ENGS:
            for sid, val in final_waits.items():
                known[e][sid] = max(known[e].get(sid, 0), val)

        block = stack.enter_context(nc.Block())

        def make(engname):
            def body(engine):
                for (waits, fn, inc) in streams[engname]:
                    for sid, val in waits.items():
                        engine.wait_ge(semobj[sid], val)
                    ins = fn(engine)
                    ins.then_inc(inc[0], inc[1])
                for sid, val in final_waits.items():
                    engine.wait_ge(semobj[sid], val)
            return body

        block.tensor(make("pe"))
        block.vector(make("dve"))
        block.scalar(make("act"))
        block.gpsimd(make("pool"))
        block.sync(make("sp"))
        self.stats = {e: len(streams[e]) for e in self.ENGS}


class SemState:
    _inst = {}
    NEPOCH = {"pe": 14, "dve": 6, "act": 6, "pool": 5, "sp": 1}

    @classmethod
    def get(cls, nc):
        if id(nc) not in cls._inst:
            cls._inst.clear()
            cls._inst[id(nc)] = cls(nc)
        return cls._inst[id(nc)]

    def __init__(self, nc):
        from contextlib import ExitStack
        self.stack = ExitStack()
        st = self.stack
        self.esems = {e: [st.enter_context(nc.semaphore(f"s_{e}_{i}")) for i in range(n)] for e, n in self.NEPOCH.items()}
        self.dsems = {e: [st.enter_context(nc.semaphore(f"d_{e}_{i}")) for i in range(NDMA_SEM)] for e in ("sp", "act", "pool")}
        self.dcount = {e: [0] * NDMA_SEM for e in self.dsems}
        self.drr = {e: 0 for e in self.dsems}
        self.ccount = {e: 0 for e in Prog.ENGS}
        self.known = {e: {} for e in Prog.ENGS}
        self.semobj = {}


from contextlib import ExitStack
import math

D = 2048
L = 4096
NCH = 8
TC = 512
KT = 16
EPS = 1e-6
TWO_PI = 6.283185


def mm(P, out, lhsT, rhs, start, stop, r, w):
    P.op("pe", lambda e, o=out, l=lhsT, rr=rhs, s=start, t=stop: e.matmul(o, l, rr, start=s, stop=t), r, w)


def tr(P, out, in_, ident, r, w):
    P.op("pe", lambda e, o=out, i=in_, d=ident: e.transpose(o, i, d), r, w)


def act(P, out, in_, func, r, w, bias=None, scale=None):
    kw = {}
    if bias is not None:
        kw["bias"] = bias
    if scale is not None:
        kw["scale"] = scale
    P.op("act", lambda e, o=out, i=in_, f=func, k=kw: e.activation(out=o, in_=i, func=f, **k), r, w)


def tt(P, eng, out, in0, in1, op, r, w):
    P.op(eng, lambda e, o=out, a=in0, b=in1, p=op: e.tensor_tensor(out=o, in0=a, in1=b, op=p), r, w)


def ts(P, eng, out, in0, s1, op0, r, w, s2=None, op1=None):
    if op1 is None:
        P.op(eng, lambda e, o=out, a=in0, x=s1, p=op0: e.tensor_scalar(out=o, in0=a, scalar1=x, scalar2=None, op0=p), r, w)
    else:
        P.op(eng, lambda e, o=out, a=in0, x=s1, y=s2, p=op0, q=op1: e.tensor_scalar(out=o, in0=a, scalar1=x, scalar2=y, op0=p, op1=q), r, w)


def stt(P, eng, out, in0, scalar, in1, op0, op1, r, w):
    P.op(eng, lambda e, o=out, a=in0, s=scalar, b=in1, p=op0, q=op1: e.scalar_tensor_tensor(out=o, in0=a, scalar=s, in1=b, op0=p, op1=q), r, w)


def cp(P, eng, out, in_, r, w):
    if eng == "act":
        P.op(eng, lambda e, o=out, i=in_: e.copy(out=o, in_=i), r, w)
    else:
        P.op(eng, lambda e, o=out, i=in_: e.tensor_copy(out=o, in_=i), r, w)


def memset(P, eng, out, val, w):
    P.op(eng, lambda e, o=out, v=val: e.memset(o, v), (), w)


class Ctx:
    pass


_UID = [0]


def sbt(nc, st, name, shape, dt):
    _UID[0] += 1
    return st.enter_context(nc.sbuf_tensor(f"s{_UID[0]}_{name}", list(shape), dt))


def pst(nc, st, name, shape, dt=None):
    _UID[0] += 1
    return st.enter_context(nc.psum_tensor(f"p{_UID[0]}_{name}", list(shape), dt or F32))


def phase_prologue(nc, G):
    with ExitStack() as st:
        P = Prog(nc)
        W = [sbt(nc, st, f"pw{i}", [128, 3072], F32) for i in range(3)]
        row = sbt(nc, st, "prow", [1, 6144], F32)
        brow = sbt(nc, st, "pbrow", [1, 6144], F32)
        one11 = sbt(nc, st, "one11", [1, 1], F32)
        banks = [pst(nc, st, f"pb{i}", [128, 512]) for i in range(8)]
        memset(P, "dve", one11[:], 1.0, ["one11"])
        P.dma("sp", G.cact[:], G.cT, writes=["cact"])
        act(P, G.cact[:], G.cact[:], AF.Silu, ["cact"], ["cact"])
        P.dma("sp", G.vecs[:], G.vecs_d, writes=["vecs"])
        P.dma("sp", G.ident[:], G.ident_d, writes=["ident"])
        P.dma("sp", G.bmask[:], G.bmask_d, writes=["bmask"])
        memset(P, "dve", G.ones_bf[:], 1.0, ["ones_bf"])
        memset(P, "dve", G.epsc[:], EPS, ["epsc"])
        emit_cast_blocks(P, G.ssm_w_in[0], 0, 8, G.wbf_in[0])
        emit_cast_blocks(P, G.ssm_w_glu[0], 0, 4, G.wbf_glu[0])
        emit_cast_blocks(P, G.ssm_w_out[0], 0, 4, G.wbf_out[0])
        slot = 0
        jobs = [(G.mod_w[l], G.mod_b[l:l + 1, :], 6144, G.modT[l]) for l in range(4)]
        jobs.append((G.kv_mod_w, G.kv_mod_b, 4096, G.kvmodT))
        for (Wd, bd, ncols, outT) in jobs:
            half = ncols // 2
            nb = half // 512
            P.dma("act", brow[0:1, 0:ncols], bd, writes=["brow"])
            for hh in range(2):
                for k in range(KT):
                    wt = W[slot % 3]
                    wk = f"pw{slot % 3}"
                    slot += 1
                    P.dma("sp" if k % 2 == 0 else "act", wt[:, 0:half], Wd[k * 128:(k + 1) * 128, hh * half:(hh + 1) * half], writes=[wk])
                    for n in range(nb):
                        mm(P, banks[n][0:1, :], G.cact[:, k:k + 1], wt[:, n * 512:(n + 1) * 512], k == 0, k == KT - 1,
                           ["cact", wk], [f"pb{n}"])
                for n in range(nb):
                    c0 = hh * half + n * 512
                    tt(P, "dve", row[0:1, c0:c0 + 512], banks[n][0:1, :], brow[0:1, c0:c0 + 512], ALU.add,
                       [f"pb{n}", "brow"], ["row"])
            nt = ncols // 128
            for j in range(nt):
                mm(P, banks[7][:, j:j + 1], row[0:1, j * 128:(j + 1) * 128], one11[0:1, 0:1], True, True,
                   ["row", "one11"], ["pb7"])
            cp(P, "dve", outT[:, 0:nt], banks[7][:, 0:nt], ["pb7"], ["modT"])
        for l in range(4):
            stt(P, "dve", G.gsc[l][:], G.modT[l][:, 16:32], 1.0, G.vecs[:, l, :], ALU.add, ALU.mult, ["modT", "vecs"], ["gsc"])
        stt(P, "dve", G.gkv[:], G.kvmodT[:, 16:32], 1.0, G.vecs[:, 4, :], ALU.add, ALU.mult, ["modT", "vecs"], ["gsc"])
        P.emit(st)


def emit_norm(P, G, xc, sq, rstd, ps_bank, ps_key, keys):
    kx, ksq, krs = keys["xc"], keys["sq"], keys["rstd"]
    act(P, sq[:], xc[:], AF.Square, [kx], [ksq])
    for k in range(KT):
        mm(P, ps_bank[:], G.ones_bf[:], sq[:, k, :], k == 0, k == KT - 1, ["ones_bf", ksq], [ps_key])
    act(P, rstd[:], ps_bank[:], AF.Sqrt, [ps_key, "epsc"], [krs], bias=G.epsc[:, 0:1], scale=1.0 / D)
    P.op("dve", lambda e, o=rstd[:], i=rstd[:]: e.reciprocal(out=o, in_=i), [krs], [krs])
    tt(P, "dve", xc[:], xc[:], rstd[:].unsqueeze(1).broadcast_to([128, KT, TC]), ALU.mult, [kx, krs], [kx])


def emit_affine(P, hT, xc, gsc, shift, kx, kh):
    for k in range(KT):
        eng = "act" if k % 2 == 0 else "dve"
        if eng == "act":
            act(P, hT[:, k, :], xc[:, k, :], AF.Identity, [kx, "gsc", "modT"], [kh], bias=shift[:, k:k + 1], scale=gsc[:, k:k + 1])
        else:
            ts(P, "dve", hT[:, k, :], xc[:, k, :], gsc[:, k:k + 1], ALU.mult, [kx, "gsc", "modT"], [kh], s2=shift[:, k:k + 1], op1=ALU.add)


def emit_cast(P, src, dst, rows=256, cols=2048):
    R, N = src.shape
    for r0 in range(0, R, rows):
        for c0 in range(0, N, cols):
            cw = min(cols, N - c0)
            P.dma("pool", dst[r0:r0 + rows, c0:c0 + cw], src[r0:r0 + rows, c0:c0 + cw], writes=["wbf_next"])


def emit_cast_blocks(P, src, c0, nblk, dst, b0=0, cw=512):
    sv = src.rearrange("(k p) c -> p k c", p=128)
    for b in range(nblk):
        P.dma("pool", dst[b0 + b].rearrange("p (k c) -> p k c", c=cw), sv[:, :, c0 + b * cw:c0 + (b + 1) * cw], writes=["wbf_next"])


def wblk(dst, b, cw=512):
    return dst[b].rearrange("p (k c) -> p k c", c=cw)


def wview(Wd, c0, cw):
    return Wd.rearrange("(k p) c -> p k c", p=128)[:, :, c0:c0 + cw]


def phase_s5_A(nc, G, l, x_src):
    with ExitStack() as st:
        P = Prog(nc)
        xc = sbt(nc, st, "xc", [128, KT, TC], F32)
        sq = sbt(nc, st, "sq", [128, KT, TC], BF16)
        hT = sbt(nc, st, "hT", [128, KT, TC], BF16)
        rstd = sbt(nc, st, "rstd", [128, TC], F32)
        wb = [sbt(nc, st, f"wb{i}", [128, KT, 512], BF16) for i in range(3)]
        og = [sbt(nc, st, f"og{i}", [128, 4, TC], BF16) for i in range(2)]
        banks = [pst(nc, st, f"ab{i}", [128, 512]) for i in range(8)]
        xv = x_src.rearrange("(k p) t -> p k t", p=128)
        uv = G.uT_scr.rearrange("(k p) t -> p k t", p=128)
        zv = G.zT_scr.rearrange("(k p) t -> p k t", p=128)
        Wd = G.wbf_in[l]
        ws = 0
        oi = 0
        bi = 0
        for ci in range(NCH):
            t0 = ci * TC
            P.dma("sp", xc[:], xv[:, :, t0:t0 + TC], writes=["xc"])
            emit_norm(P, G, xc, sq, rstd, banks[7], "ab7", {"xc": "xc", "sq": "sq", "rstd": "rstd"})
            emit_affine(P, hT, xc, G.gsc[l], G.modT[l][:, 0:16], "xc", "hT")
            for cg in range(8):
                w = wb[ws % 3]
                wk = f"wb{ws % 3}"
                ws += 1
                P.dma("sp", w[:], wblk(Wd, cg), reads=["wbf"], writes=[wk])
                o = og[oi % 2]
                ok = f"og{oi % 2}"
                oi += 1
                for j in range(4):
                    b = banks[bi % 6]
                    bk = f"ab{bi % 6}"
                    bi += 1
                    for k in range(KT):
                        mm(P, b[:], w[:, k, j * 128:(j + 1) * 128], hT[:, k, :], k == 0, k == KT - 1, [wk, "hT"], [bk])
                    if cg < 4:
                        cp(P, "dve" if j % 2 else "act", o[:, j, :], b[:], [bk], [ok])
                    else:
                        act(P, o[:, j, :], b[:], AF.Silu, [bk], [ok])
                dst = uv if cg < 4 else zv
                cgl = cg % 4
                P.dma("pool", dst[:, cgl * 4:(cgl + 1) * 4, t0:t0 + TC], o[:], reads=[ok], writes=["uz_scr"])
        P.emit(st)


def phase_s5_C(nc, G, l, x_src, x_dst):
    with ExitStack() as st:
        P = Prog(nc)
        xcs = [sbt(nc, st, f"xc{i}", [128, KT, TC], F32) for i in range(2)]
        ygs = [sbt(nc, st, f"yg{i}", [128, KT, TC], BF16) for i in range(2)]
        zss = [sbt(nc, st, f"zs{i}", [128, KT, TC], BF16) for i in range(2)]
        mT = sbt(nc, st, "mT", [128, KT, TC], BF16)
        sg = [sbt(nc, st, f"sg{i}", [128, TC], F32) for i in range(2)]
        wb = [sbt(nc, st, f"wb{i}", [128, KT, 512], BF16) for i in range(3)]
        banks = [pst(nc, st, f"cb{i}", [128, 512]) for i in range(8)]
        xv = x_src.rearrange("(k p) t -> p k t", p=128)
        xo = x_dst.rearrange("(k p) t -> p k t", p=128)
        yv = G.ygT_scr.rearrange("(k p) t -> p k t", p=128)
        zv = G.zT_scr.rearrange("(k p) t -> p k t", p=128)
        bglu = G.vecs[:, 8 + l, :]
        gate = G.modT[l][:, 32:48]
        ws = 0
        bi = 0
        si = 0
        for ci in range(NCH):
            t0 = ci * TC
            xc, yg, zs = xcs[ci % 2], ygs[ci % 2], zss[ci % 2]
            kxc, kyg, kzs = f"xc{ci % 2}", f"yg{ci % 2}", f"zs{ci % 2}"
            P.dma("sp", yg[:], yv[:, :, t0:t0 + TC], reads=["yg_scr"], writes=[kyg])
            P.dma("sp", zs[:], zv[:, :, t0:t0 + TC], reads=["uz_scr"], writes=[kzs])
            P.dma("sp", xc[:], xv[:, :, t0:t0 + TC], writes=[kxc])
            for cg in range(4):
                w = wb[ws % 3]
                wk = f"wb{ws % 3}"
                ws += 1
                P.dma("sp", w[:], wblk(G.wbf_glu[l], cg), reads=["wbf"], writes=[wk])
                for j in range(4):
                    ct = cg * 4 + j
                    b = banks[bi % 8]
                    bk = f"cb{bi % 8}"
                    bi += 1
                    for k in range(KT):
                        mm(P, b[:], w[:, k, j * 128:(j + 1) * 128], yg[:, k, :], k == 0, k == KT - 1, [wk, kyg], [bk])
                    s = sg[si % 2]
                    sk = f"sg{si % 2}"
                    si += 1
                    act(P, s[:], b[:], AF.Sigmoid, [bk, "vecs"], [sk], bias=bglu[:, ct:ct + 1], scale=1.0)
                    tt(P, "dve", s[:], s[:], yg[:, ct, :], ALU.mult, [sk, kyg], [sk])
                    tt(P, "pool", mT[:, ct, :], s[:], zs[:, ct, :], ALU.mult, [sk, kzs], ["mT"])
            for cg in range(4):
                w = wb[ws % 3]
                wk = f"wb{ws % 3}"
                ws += 1
                P.dma("sp", w[:], wblk(G.wbf_out[l], cg), reads=["wbf"], writes=[wk])
                for j in range(4):
                    ct = cg * 4 + j
                    b = banks[bi % 8]
                    bk = f"cb{bi % 8}"
                    bi += 1
                    for k in range(KT):
                        mm(P, b[:], w[:, k, j * 128:(j + 1) * 128], mT[:, k, :], k == 0, k == KT - 1, [wk, "mT"], [bk])
                    stt(P, "dve", xc[:, ct, :], b[:], gate[:, ct:ct + 1], xc[:, ct, :], ALU.mult, ALU.add, [bk, "modT", kxc], [kxc])
            P.dma("pool", xo[:, :, t0:t0 + TC], xc[:], reads=[kxc], writes=["x_scr"])
        P.emit(st)


T1 = 16
NJ = L // T1
NLEV = 8


def emit_disc(P, nc, st, tag, F, lam, a_out, coef_out=None):
    kl, ka, kc = "lam" + tag, "a" + tag, "coef" + tag
    tm = [sbt(nc, st, f"dt{tag}{i}", [128, F], F32) for i in range(6)]
    ti = sbt(nc, st, f"di{tag}", [128, F], I32)
    K = [f"dt{tag}{i}" for i in range(6)]
    lr, li, ls = lam[:, 0, :], lam[:, 1, :], lam[:, 2, :]
    dt, lrdt, y, r, mag, sc = [t[:] for t in tm]
    act(P, dt, ls, AF.Exp, [kl], [K[0]])
    tt(P, "dve", lrdt, lr, dt, ALU.mult, [kl, K[0]], [K[1]])
    act(P, mag, lrdt, AF.Exp, [K[1]], [K[4]])
    tt(P, "dve", y, li, dt, ALU.mult, [kl, K[0]], [K[2]])
    ts(P, "dve", y, y, 1.0 / (2.0 * math.pi), ALU.mult, [K[2]], [K[2]])
    for which in range(2):
        if which == 1:
            ts(P, "dve", y, y, 0.25, ALU.add, [K[2]], [K[2]])
        cp(P, "dve", ti[:], y, [K[2]], ["di" + tag])
        cp(P, "dve", r, ti[:], ["di" + tag], [K[3]])
        tt(P, "dve", r, y, r, ALU.subtract, [K[2], K[3]], [K[3]])
        act(P, sc, r, AF.Sin, [K[3]], [K[5]], scale=TWO_PI)
        tt(P, "dve", a_out[:, 1 - which, :], mag, sc, ALU.mult, [K[4], K[5]], [ka])
    if coef_out is not None:
        den, am1, t1, t2 = dt, lrdt, y, r
        tt(P, "dve", den, lr, lr, ALU.mult, [kl], [K[0]])
        tt(P, "dve", t1, li, li, ALU.mult, [kl], [K[2]])
        tt(P, "dve", den, den, t1, ALU.add, [K[0], K[2]], [K[0]])
        P.op("dve", lambda e, o=den, i=den: e.reciprocal(out=o, in_=i), [K[0]], [K[0]])
        ts(P, "dve", am1, a_out[:, 0, :], -1.0, ALU.add, [ka], [K[1]])
        tt(P, "dve", t1, am1, lr, ALU.mult, [K[1], kl], [K[2]])
        tt(P, "dve", t2, a_out[:, 1, :], li, ALU.mult, [ka, kl], [K[3]])
        tt(P, "dve", t1, t1, t2, ALU.add, [K[2], K[3]], [K[2]])
        tt(P, "dve", coef_out[:, 0, :], t1, den, ALU.mult, [K[2], K[0]], [kc])
        tt(P, "dve", t1, a_out[:, 1, :], lr, ALU.mult, [ka, kl], [K[2]])
        tt(P, "dve", t2, am1, li, ALU.mult, [K[1], kl], [K[3]])
        tt(P, "dve", t1, t1, t2, ALU.subtract, [K[2], K[3]], [K[2]])
        tt(P, "dve", coef_out[:, 1, :], t1, den, ALU.mult, [K[2], K[0]], [kc])


def cmul(P, eng, o_re, o_im, x_re, x_im, s_re, s_im, t1, t2, r, w, tk):
    tt(P, eng, t1, x_im, s_im, ALU.mult, r, [tk + "1"])
    tt(P, eng, t2, x_im, s_re, ALU.mult, r, [tk + "2"])
    tt(P, eng, o_re, x_re, s_re, ALU.mult, r, w)
    tt(P, eng, o_im, x_re, s_im, ALU.mult, r, w)
    tt(P, eng, o_re, o_re, t1, ALU.subtract, list(w) + [tk + "1"], w)
    tt(P, eng, o_im, o_im, t2, ALU.add, list(w) + [tk + "2"], w)


def emit_pow17(P, pw, shape_mid, key, t1, t2, tk):
    raise NotImplementedError


def phase_s5_setup(nc, G, S, l):
    with ExitStack() as st:
        P = Prog(nc)
        lamgn = sbt(nc, st, "lamgn", [128, 3, 64], F32)
        lamch = sbt(nc, st, "lamch", [128, 3, 1024], F32)
        bgn = sbt(nc, st, "bgn", [128, 2, 64, 16], F32)
        a_gn = sbt(nc, st, "a_gn", [128, 2, 64], F32)
        coef = sbt(nc, st, "coefgn", [128, 2, 64], F32)
        tA = sbt(nc, st, "tA", [128, 64, 16], F32)
        tB = sbt(nc, st, "tB", [128, 64, 16], F32)
        P.dma("sp", lamgn[:], G.lamgn_d[l], writes=["lamgn"])
        P.dma("sp", lamch[:], G.lamch_d[l], writes=["lamch"])
        P.dma("act", bgn[:], G.bgn_d[l], writes=["bgn"])
        emit_disc(P, nc, st, "gn", 64, lamgn, a_gn, coef)
        emit_disc(P, nc, st, "ch", 1024, lamch, S.a_ch, None)
        cre = coef[:, 0, :].unsqueeze(2).broadcast_to([128, 64, 16])
        cim = coef[:, 1, :].unsqueeze(2).broadcast_to([128, 64, 16])
        cmul(P, "dve", S.Bbar[:, 0], S.Bbar[:, 1], bgn[:, 0], bgn[:, 1], cre, cim, tA[:], tB[:],
             ["bgn", "coefgn"], ["Bbar"], "tAB")
        ap_ = S.apow
        memset(P, "dve", ap_[:, 0, :, 0:1], 1.0, ["apow"])
        memset(P, "dve", ap_[:, 1, :, 0:1], 0.0, ["apow"])
        cp(P, "dve", ap_[:, 0, :, 1:2], a_gn[:, 0, :].unsqueeze(2), ["agn", "apow"], ["apow"])
        cp(P, "dve", ap_[:, 1, :, 1:2], a_gn[:, 1, :].unsqueeze(2), ["agn", "apow"], ["apow"])
        m = 1
        while m < 16:
            sre = ap_[:, 0, :, m:m + 1].broadcast_to([128, 64, m])
            sim = ap_[:, 1, :, m:m + 1].broadcast_to([128, 64, m])
            cmul(P, "dve", ap_[:, 0, :, m + 1:2 * m + 1], ap_[:, 1, :, m + 1:2 * m + 1],
                 ap_[:, 0, :, 1:m + 1], ap_[:, 1, :, 1:m + 1], sre, sim,
                 tA[:, :, 0:m], tB[:, :, 0:m], ["apow"], ["apow"], "tAB")
            m *= 2
        hs = S.hsA
        cp(P, "dve", hs[:, 0, :, 0:1], ap_[:, 0, :, 16:17], ["apow"], ["hsA"])
        cp(P, "dve", hs[:, 1, :, 0:1], ap_[:, 1, :, 16:17], ["apow"], ["hsA"])
        for lev in range(1, NLEV):
            cmul(P, "dve", hs[:, 0, :, lev:lev + 1], hs[:, 1, :, lev:lev + 1],
                 hs[:, 0, :, lev - 1:lev], hs[:, 1, :, lev - 1:lev], hs[:, 0, :, lev - 1:lev], hs[:, 1, :, lev - 1:lev],
                 tA[:, :, 0:1], tB[:, :, 0:1], ["hsA"], ["hsA"], "tAB")
        ts(P, "dve", S.hsAn[:], hs[:, 1], -1.0, ALU.mult, ["hsA"], ["hsA"])
        P.emit(st)


def phase_s5_B(nc, G, S, l):
    with ExitStack() as st:
        P = Prog(nc)
        cch = sbt(nc, st, "cch", [128, 2, 64], F32)
        apc = sbt(nc, st, "apc", [128, 2, 17, 64], F32)
        E = sbt(nc, st, "E", [128, 17, 2, 64], F32)
        Wt = sbt(nc, st, "Wt", [128, 16, 4, 2, 16], F32)
        tA = sbt(nc, st, "tA", [128, 17 * 64], F32)
        tB = sbt(nc, st, "tB", [128, 17 * 64], F32)
        Xm = [sbt(nc, st, f"Xm{i}", [128, 8, 128], F32) for i in range(2)]
        QW0 = sbt(nc, st, "QW0", [128, 8, 128], BF16)
        Bm = sbt(nc, st, "Bm", [128, 8, 128], BF16)
        PWs = sbt(nc, st, "PWs", [128, 4, 16, 2, 128], BF16)
        QWs = sbt(nc, st, "QWs", [128, 4, 16, 2, 128], BF16)
        KW = sbt(nc, st, "KW", [128, 16, 128], BF16)
        ygT = [sbt(nc, st, f"ygT{i}", [128, L], BF16) for i in range(1)]
        Z = [sbt(nc, st, f"Z{i}", [128, 4, 2, 128 + NJ], F32) for i in range(2)]
        H1 = sbt(nc, st, "H1", [128, 4, 2, NJ], F32)
        Xbf = sbt(nc, st, "Xbf", [128, 4, 2, NJ], BF16)
        uS = sbt(nc, st, "uS", [128, T1, NJ], BF16)
        banks = [pst(nc, st, f"bb{i}", [128, 512]) for i in range(8)]
        dcol = G.vecs[:, 6 + l, :]
        memset(P, "dve", Xbf[:, :, :, 0:1], 0.0, ["Xbf"])
        for zi_ in range(2):
            memset(P, "dve", Z[zi_][:, :, :, 0:128], 0.0, [f"Z{zi_}_{q}" for q in range(4)])
        if l == 0:
            emit_cast_blocks(P, G.ssm_w_in[1], 0, 8, G.wbf_in[1])
            emit_cast_blocks(P, G.ssm_w_glu[1], 0, 4, G.wbf_glu[1])
            emit_cast_blocks(P, G.ssm_w_out[1], 0, 4, G.wbf_out[1])
        else:
            emit_cast_blocks(P, G.w_kv, 0, 6, G.wbf_kv)
            emit_cast_qg(P, G, 0)
        uv = G.uT_scr.rearrange("(k p) t -> p k t", p=128)
        yv = G.ygT_scr.rearrange("(k p) t -> p k t", p=128)
        bmask3 = G.bmask[:].rearrange("p (a b) -> p a b", b=16)
        xi = 0
        for ct in range(KT):
            u = ygT[0]
            uk = "ygT0"
            P.dma("sp", cch[:], G.cch_d[l][:, :, ct, :], writes=["cch"])
            memset(P, "pool", apc[:, 0, 0:1, :], 1.0, ["apc"])
            memset(P, "pool", apc[:, 1, 0:1, :], 0.0, ["apc"])
            cp(P, "pool", apc[:, 0, 1:2, :], S.a_ch[:, 0, ct * 64:(ct + 1) * 64].unsqueeze(1), ["ach", "apc"], ["apc"])
            cp(P, "pool", apc[:, 1, 1:2, :], S.a_ch[:, 1, ct * 64:(ct + 1) * 64].unsqueeze(1), ["ach", "apc"], ["apc"])
            m = 1
            tA3 = tA[:].rearrange("p (k n) -> p k n", n=64)
            tB3 = tB[:].rearrange("p (k n) -> p k n", n=64)
            while m < 16:
                sre = apc[:, 0, m:m + 1, :].broadcast_to([128, m, 64])
                sim = apc[:, 1, m:m + 1, :].broadcast_to([128, m, 64])
                cmul(P, "pool", apc[:, 0, m + 1:2 * m + 1, :], apc[:, 1, m + 1:2 * m + 1, :],
                     apc[:, 0, 1:m + 1, :], apc[:, 1, 1:m + 1, :], sre, sim,
                     tA3[:, 0:m, :], tB3[:, 0:m, :], ["apc"], ["apc"], "tAB")
                m *= 2
            cre = cch[:, 0, :].unsqueeze(1).broadcast_to([128, 17, 64])
            cim = cch[:, 1, :].unsqueeze(1).broadcast_to([128, 17, 64])
            cmul(P, "pool", E[:, :, 0, :], E[:, :, 1, :], apc[:, 0], apc[:, 1], cre, cim, tA3, tB3,
                 ["apc", "cch"], ["E"], "tAB")
            ts(P, "pool", E[:, :, 1, :], E[:, :, 1, :], -1.0, ALU.mult, ["E"], ["E"])
            P.dma("sp", u[:], uv[:, ct, :], reads=["uz_scr"], writes=[uk])
            cp(P, "pool", uS[:], u[:].rearrange("p (j s) -> p s j", s=T1), [uk], ["uS"])
            for ri_o in range(2):
                pass
            T0 = ct * 4
            akre = S.apow[:, 0, T0:T0 + 4, 0:16].rearrange("p t k -> p k t").unsqueeze(3).broadcast_to([128, 16, 4, 16])
            akim = S.apow[:, 1, T0:T0 + 4, 0:16].rearrange("p t k -> p k t").unsqueeze(3).broadcast_to([128, 16, 4, 16])
            bre = S.Bbar[:, 0, T0:T0 + 4, :].unsqueeze(1).broadcast_to([128, 16, 4, 16])
            bim = S.Bbar[:, 1, T0:T0 + 4, :].unsqueeze(1).broadcast_to([128, 16, 4, 16])
            tA4 = tA[:, 0:1024].rearrange("p (k t c) -> p k t c", k=16, t=4)
            tB4 = tB[:, 0:1024].rearrange("p (k t c) -> p k t c", k=16, t=4)
            cmul(P, "dve", Wt[:, :, :, 0, :], Wt[:, :, :, 1, :], akre, akim, bre, bim, tA4, tB4,
                 ["apow", "Bbar"], ["Wt"], "tAB")
            if getattr(G, "dbgB", None) is not None and ct == G.dbg_ct:
                P.dma("sp", G.dbgB[:, 0:2176], E[:].rearrange("p k r n -> p (k r n)"), reads=["E"], writes=["dbgB"])
                P.dma("sp", G.dbgB[:, 2176:2176 + 2048], Wt[:].rearrange("p k t r c -> p (k t r c)"), reads=["Wt"], writes=["dbgB"])
                P.dma("sp", G.dbgB[:, 4224:4224 + 128], cch[:].rearrange("p r n -> p (r n)"), reads=["cch"], writes=["dbgB"])
            for k in range(17):
                xm = Xm[xi % 2]
                xk = f"Xm{xi % 2}"
                xi += 1
                src = E[:, k].rearrange("p r (q n) -> p (r q) n", n=16)
                tt(P, "dve", xm[:].rearrange("p a (g n) -> p a g n", n=16),
                   src.unsqueeze(2).broadcast_to([128, 8, 8, 16]),
                   bmask3.unsqueeze(1).broadcast_to([128, 8, 8, 16]), ALU.mult, ["E", "bmask"], [xk])
                pb = (k % 2) * 2
                for a in range(8):
                    bk = banks[pb + a // 4]
                    tr(P, bk[:, (a % 4) * 128:(a % 4 + 1) * 128], xm[:, a, :], G.ident[:], [xk, "ident"], [f"bb{pb + a // 4}"])
                for hb in range(2):
                    src_ps = banks[pb + hb][:].rearrange("p (q n) -> p q n", n=128)
                    if k == 0:
                        cp(P, "act", QW0[:, hb * 4:(hb + 1) * 4, :], src_ps, [f"bb{pb + hb}"], ["QW0"])
                    else:
                        cp(P, "act", QWs[:, :, k - 1, hb, :], src_ps, [f"bb{pb + hb}"], ["QWs"])
            T0 = ct * 4
            for ri in range(2):
                tt(P, "dve", Bm[:].rearrange("p (q r) (g c) -> p q r g c", r=2, c=16)[:, :, ri],
                   S.Bbar[:, ri, T0:T0 + 4, :].unsqueeze(2).broadcast_to([128, 4, 8, 16]),
                   bmask3.unsqueeze(1).broadcast_to([128, 4, 8, 16]), ALU.mult, ["Bbar", "bmask"], ["Bm"])
            for tau in range(16):
                cb = banks[tau % 2]
                cbk = f"bb{tau % 2}"
                n_mm = 0
                for nq in range(4):
                    for ri in range(2):
                        rhs = QW0[:, ri * 4 + nq, :] if tau == 0 else QWs[:, nq, tau - 1, ri, :]
                        mm(P, cb[:, 0:128], Bm[:, nq * 2 + ri, :], rhs, n_mm == 0, n_mm == 7, ["Bm", "QW0", "QWs"], [cbk])
                        n_mm += 1
                if tau == 0:
                    stt(P, "dve", KW[:, 0, :], G.ident[:], dcol[:, ct:ct + 1], cb[:, 0:128], ALU.mult, ALU.add,
                        [cbk, "ident", "vecs"], ["KW"])
                else:
                    cp(P, "dve", KW[:, tau, :], cb[:, 0:128], [cbk], ["KW"])
            for k in range(16):
                xm = Xm[xi % 2]
                xk = f"Xm{xi % 2}"
                xi += 1
                src = Wt[:, k].rearrange("p t r c -> p (t r) c")
                tt(P, "dve", xm[:].rearrange("p a (g n) -> p a g n", n=16),
                   src.unsqueeze(2).broadcast_to([128, 8, 8, 16]),
                   bmask3.unsqueeze(1).broadcast_to([128, 8, 8, 16]), ALU.mult, ["Wt", "bmask"], [xk])
                pb = 4 + (k % 2) * 2
                for a in range(8):
                    bk = banks[pb + a // 4]
                    tr(P, bk[:, (a % 4) * 128:(a % 4 + 1) * 128], xm[:, a, :], G.ident[:], [xk, "ident"], [f"bb{pb + a // 4}"])
                for hb in range(2):
                    src_ps = banks[pb + hb][:].rearrange("p (q r n) -> p q r n", r=2, n=128)
                    cp(P, "act", PWs[:, 2 * hb:2 * hb + 2, 15 - k, :, :], src_ps, [f"bb{pb + hb}"], ["PWs"])
            z0 = Z[0]
            u3 = u[:].rearrange("p (j s) -> p j s", s=T1)
            for nq in range(4):
                b = banks[nq]
                bk = f"bb{nq}"
                for ri in range(2):
                    for s in range(T1):
                        mm(P, b[:, ri * NJ:(ri + 1) * NJ], PWs[:, nq, s, ri, :], uS[:, s, :], s == 0, s == T1 - 1,
                           ["PWs", "uS"], [bk])
                cp(P, "act", z0[:, nq, :, 128:128 + NJ], b[:].rearrange("p (r j) -> p r j", r=2), [bk], [f"Z0_{nq}"])
            yg = ygT[0]
            ygk = "ygT0"
            yg3 = yg[:].rearrange("p (j s) -> p s j", s=T1)
            for s in range(T1):
                bnk = banks[s // 2]
                bk = f"bb{s // 2}"
                o = bnk[:, (s % 2) * NJ:(s % 2 + 1) * NJ]
                for tau in range(s + 1):
                    P.op("pe", lambda e, oo=o, l_=KW[:, tau, :], r_=uS[:, s - tau, :], st_=(tau == 0 and s % 2 == 0): e.matmul(oo, l_, r_, start=st_, stop=False, skip_group_check=True),
                         ["KW", "uS"], [bk])
            cur = 0
            T0 = ct * 4
            for lev in range(NLEV):
                d = 1 << lev
                zi, zo = Z[cur], Z[1 - cur]
                ki, ko = f"Z{cur}", f"Z{1 - cur}"
                for nq in range(4):
                    Ar = S.hsA[:, 0, T0 + nq, lev:lev + 1]
                    stt(P, "dve", H1[:, nq, :, :], zi[:, nq, :, 128 - d:128 - d + NJ], Ar, zi[:, nq, :, 128:128 + NJ], ALU.mult, ALU.add,
                        [f"{ki}_{nq}", "hsA"], [f"H1_{nq}"])
                for nq in range(4):
                    Ai = S.hsA[:, 1, T0 + nq, lev:lev + 1]
                    An = S.hsAn[:, T0 + nq, lev:lev + 1]
                    stt(P, "dve", zo[:, nq, 0, 128:128 + NJ], zi[:, nq, 1, 128 - d:128 - d + NJ], An, H1[:, nq, 0, :], ALU.mult, ALU.add,
                        [f"{ki}_{nq}", "hsA", f"H1_{nq}"], [f"{ko}_{nq}"])
                    stt(P, "dve", zo[:, nq, 1, 128:128 + NJ], zi[:, nq, 0, 128 - d:128 - d + NJ], Ai, H1[:, nq, 1, :], ALU.mult, ALU.add,
                        [f"{ki}_{nq}", "hsA", f"H1_{nq}"], [f"{ko}_{nq}"])
                cur = 1 - cur
            zf = Z[cur]
            kf = f"Z{cur}"
            cp(P, "act", Xbf[:, :, :, 1:NJ], zf[:, :, :, 128:128 + NJ - 1], [f"{kf}_{q}" for q in range(4)], ["Xbf"])
            for s in range(T1):
                bnk = banks[s // 2]
                bk = f"bb{s // 2}"
                o = bnk[:, (s % 2) * NJ:(s % 2 + 1) * NJ]
                n_mm = 0
                for nq in range(4):
                    for ri in range(2):
                        P.op("pe", lambda e, oo=o, l_=QWs[:, nq, s, ri, :], r_=Xbf[:, nq, ri, :], sp_=(n_mm == 7): e.matmul(oo, l_, r_, start=False, stop=sp_, skip_group_check=True),
                             ["QWs", "Xbf"], [bk])
                        n_mm += 1
                if s % 2 == 1:
                    act(P, yg3[:, s - 1:s + 1, :], bnk[:].rearrange("p (s j) -> p s j", s=2), AF.Gelu_apprx_tanh, [bk], [ygk])
            P.dma("act", yv[:, ct, :], yg[:], reads=[ygk], writes=["yg_scr"])
        P.emit(st)


NCMP = 255
QSCALE = 128 ** -0.5
TINY = 1e-30


def emit_cast_qg(P, G, jl):
    emit_cast_blocks(P, G.nsa_w_qg[jl], 0, 8, G.wbf_qg[jl], b0=0, cw=256)
    emit_cast_blocks(P, G.nsa_w_qg[jl], 2096, 24, G.wbf_qg[jl], b0=8, cw=256)
    emit_cast_blocks(P, G.nsa_w_qg[jl], 2048, 1, G.wbf_g[jl], b0=0, cw=48)
    emit_cast_blocks(P, G.nsa_w_o[jl], 0, 8, G.wbf_o[jl], cw=256)


def phase_kv(nc, G, x_src):
    with ExitStack() as st:
        P = Prog(nc)
        xc = sbt(nc, st, "xc", [128, KT, TC], F32)
        sq = sbt(nc, st, "sq", [128, KT, TC], BF16)
        hT = sbt(nc, st, "hT", [128, KT, TC], BF16)
        rstd = sbt(nc, st, "rstd", [128, TC], F32)
        wb = [sbt(nc, st, f"wb{i}", [128, KT, 512], BF16) for i in range(3)]
        og = [sbt(nc, st, f"og{i}", [128, 4, TC], BF16) for i in range(2)]
        banks = [pst(nc, st, f"kb{i}", [128, 512]) for i in range(8)]
        xv = x_src.rearrange("(k p) t -> p k t", p=128)
        ws = oi = bi = 0
        fm_slices = [0, 1, 2, 4]
        for ci in range(NCH):
            t0 = ci * TC
            P.dma("sp", xc[:], xv[:, :, t0:t0 + TC], writes=["xc"])
            emit_norm(P, G, xc, sq, rstd, banks[7], "kb7", {"xc": "xc", "sq": "sq", "rstd": "rstd"})
            emit_affine(P, hT, xc, G.gkv, G.kvmodT[:, 0:16], "xc", "hT")
            for sl in range(6):
                w = wb[ws % 3]
                wk = f"wb{ws % 3}"
                ws += 1
                P.dma("sp", w[:], wblk(G.wbf_kv, sl), reads=["wbf"], writes=[wk])
                o = og[oi % 2]
                ok = f"og{oi % 2}"
                oi += 1
                if sl in fm_slices:
                    for g in range(4):
                        b = banks[bi % 6]
                        bk = f"kb{bi % 6}"
                        bi += 1
                        for k in range(KT):
                            mm(P, b[:], w[:, k, g * 128:(g + 1) * 128], hT[:, k, :], k == 0, k == KT - 1, [wk, "hT"], [bk])
                        cp(P, "dve" if g % 2 else "act", o[:, g, :], b[:], [bk], [ok])
                    fi = fm_slices.index(sl)
                    P.dma("pool", G.kvT_scr[fi].rearrange("g p t -> p g t")[:, :, t0:t0 + TC], o[:], reads=[ok], writes=["kv_scr"])
                else:
                    vi = 0 if sl == 3 else 1
                    for tt_ in range(4):
                        b = banks[bi % 6]
                        bk = f"kb{bi % 6}"
                        bi += 1
                        for k in range(KT):
                            mm(P, b[:], hT[:, k, tt_ * 128:(tt_ + 1) * 128], w[:, k, :], k == 0, k == KT - 1, [wk, "hT"], [bk])
                        cp(P, "dve" if tt_ % 2 else "act", o[:, tt_, :], b[:], [bk], [ok])
                    for g in range(4):
                        P.dma("pool", G.vtok_scr[vi, g].rearrange("p (a d) -> p a d", d=128)[:, ci * 4:(ci + 1) * 4, :],
                              o[:, :, g * 128:(g + 1) * 128], reads=[ok], writes=["kv_scr"])
        P.emit(st)


def phase_cmp(nc, G):
    with ExitStack() as st:
        P = Prog(nc)
        src = [sbt(nc, st, f"csrc{i}", [128, L], BF16) for i in range(2)]
        w1 = [sbt(nc, st, f"cw1{i}", [128, 32, 128], BF16) for i in range(2)]
        w2 = [sbt(nc, st, f"cw2{i}", [128, 128], BF16) for i in range(2)]
        pe32 = sbt(nc, st, "cpe32", [128, 2, 32], F32)
        pe16 = sbt(nc, st, "cpe16", [128, 2, 32], BF16)
        b1 = sbt(nc, st, "cb1", [128, 2], F32)
        b2c = sbt(nc, st, "cb2c", [128, 2], F32)
        b2r = sbt(nc, st, "cb2r", [128, 128], F32)
        bias1 = sbt(nc, st, "cbias1", [128, 2], F32)
        hid = sbt(nc, st, "chid", [128, 256], BF16)
        banks = [pst(nc, st, f"cpb{i}", [128, 512]) for i in range(4)]
        P.dma("sp", pe32[:], G.cmp_peT, writes=["pe32"])
        cp(P, "dve", pe16[:], pe32[:], ["pe32"], ["pe16"])
        P.dma("sp", b1[:], G.cmp_b1T, writes=["b1"])
        P.dma("sp", b2c[:], G.cmp_b2T, writes=["b2c"])
        P.dma("sp", b2r[:], G.cmp_b2row, writes=["b2r"])
        memset(P, "dve", G.kcT[:], 0.0, ["kcT"])
        memset(P, "dve", G.vc[:], 0.0, ["vc"])
        memset(P, "dve", hid[:], 0.0, ["hid"])
        for i in range(2):
            P.dma("pool", w1[i][:], G.cmp_w1[i].rearrange("(j d) o -> d j o", d=128), writes=[f"w1{i}"])
            P.dma("pool", w2[i][:], G.cmp_w2[i], writes=[f"w2{i}"])
            for j in range(32):
                mm(P, banks[3][:, 0:1], w1[i][:, j, :], pe16[:, i, j:j + 1], j == 0, j == 31, [f"w1{i}", "pe16"], ["cpb3"])
            tt(P, "dve", bias1[:, i:i + 1], banks[3][:, 0:1], b1[:, i:i + 1], ALU.add, ["cpb3", "b1"], ["bias1"])
        n_it = 0
        for g in range(4):
            for i in range(2):
                s = src[n_it % 2]
                sk = f"csrc{n_it % 2}"
                b = banks[n_it % 2]
                bk = f"cpb{n_it % 2}"
                n_it += 1
                P.dma("sp", s[:], G.kvT_scr[i, g], reads=["kv_scr"], writes=[sk])
                for j in range(32):
                    rhs = s[:, j:j + 16 * (NCMP - 1) + 1:16]
                    mm(P, b[:, 0:NCMP], w1[i][:, j, :], rhs, j == 0, j == 31, [f"w1{i}", sk], [bk])
                act(P, hid[:, 0:NCMP], b[:, 0:NCMP], AF.Gelu_apprx_tanh, [bk, "bias1"], ["hid"], bias=bias1[:, i:i + 1], scale=1.0)
                if i == 0:
                    mm(P, banks[2][:, 0:NCMP], w2[0][:], hid[:, 0:NCMP], True, True, ["w20", "hid"], ["cpb2"])
                    act(P, G.kcT[:, g, 0:NCMP], banks[2][:, 0:NCMP], AF.Identity, ["cpb2", "b2c"], ["kcT"], bias=b2c[:, 0:1], scale=1.0)
                else:
                    for nt in range(2):
                        nn = 128 if nt == 0 else NCMP - 128
                        mm(P, banks[2][0:nn, nt * 128:(nt + 1) * 128], hid[:, nt * 128:nt * 128 + nn], w2[1][:], True, True,
                           ["w21", "hid"], ["cpb2"])
                        tt(P, "dve", G.vc[0:nn, g, nt, :], banks[2][0:nn, nt * 128:(nt + 1) * 128], b2r[0:nn, :], ALU.add,
                           ["cpb2", "b2r"], ["vc"])
        P.emit(st)


def phase_nsa(nc, G, layer, x_src):
    jl = layer - 2
    Wq = G.wbf_qg[jl]
    with ExitStack() as st:
        P = Prog(nc)
        big1 = sbt(nc, st, "big1", [128, KT * TC * 4], mybir.dt.uint8)
        xc = big1[:].bitcast(F32).rearrange("p (k t) -> p k t", t=TC)
        maskT = big1[:].bitcast(BF16).rearrange("p (k t) -> p k t", t=TC)
        big2 = sbt(nc, st, "big2", [128, KT, TC], BF16)
        hT = sbt(nc, st, "hT", [128, KT, TC], BF16)
        rstd = sbt(nc, st, "rstd", [128, TC], F32)
        wb = [sbt(nc, st, f"wb{i}", [128, KT, 256], BF16) for i in range(4)]
        wg = sbt(nc, st, "wg", [128, KT, 48], BF16)
        qT = sbt(nc, st, "qT", [128, 4, TC], BF16)
        zT = sbt(nc, st, "zT", [128, 12, TC], BF16)
        gT = sbt(nc, st, "gT", [48, TC], F32)
        grep_ = [sbt(nc, st, f"grep{i}", [128, TC], F32) for i in range(4)]
        ksT = sbt(nc, st, "ksT", [128, L], BF16)
        vs = sbt(nc, st, "vs", [128, 32, 128], BF16)
        kwT = sbt(nc, st, "kwT", [128, 1024], BF16)
        vw = sbt(nc, st, "vw", [128, 8, 128], BF16)
        Pe = [sbt(nc, st, f"Pe{i}", [128, TC], BF16) for i in range(5)]
        Pm = [sbt(nc, st, f"Pm{i}", [128, TC], BF16) for i in range(5)]
        pc = sbt(nc, st, "pc", [128, 2, TC], F32)
        ptmp = sbt(nc, st, "ptmp", [128, TC], F32)
        rinv = [sbt(nc, st, f"rinv{i}", [128, TC], F32) for i in range(2)]
        o1 = [sbt(nc, st, f"o1{i}", [128, TC], F32) for i in range(2)]
        oacc = sbt(nc, st, "oacc", [128, TC], F32)
        cmk = sbt(nc, st, "cmk", [128, 2, TC], BF16)
        cau = sbt(nc, st, "cau", [128, 4, TC], BF16)
        win = sbt(nc, st, "win", [128, 8, TC], BF16)
        Esel = sbt(nc, st, "Esel", [64, 32, 128], BF16)
        ov = sbt(nc, st, "ov", [128, 2, 64], F32)
        selA = sbt(nc, st, "selA", [128, 4, 64], F32)
        selB = sbt(nc, st, "selB", [128, 4, 64], F32)
        score = sbt(nc, st, "score", [128, 4, 64], F32)
        work = sbt(nc, st, "work", [128, 4, 64], F32)
        m8 = sbt(nc, st, "m8", [128, 4, 16], F32)
        self_ = sbt(nc, st, "self", [128, 4, 64], F32)
        selTb = sbt(nc, st, "selTb", [64, TC], BF16)
        tinyc = sbt(nc, st, "tinyc", [128, 1], F32)
        banks = [pst(nc, st, f"nb{i}", [128, 512]) for i in range(8)]
        memset(P, "dve", tinyc[:], TINY, ["tinyc"])
        xv = x_src.rearrange("(k p) t -> p k t", p=128)
        xo = G.xres.rearrange("(k p) t -> p k t", p=128)
        gate = G.modT[layer][:, 32:48]
        if layer == 2:
            emit_cast_qg(P, G, 1)
        P.dma("sp", cau[:], G.cauM.rearrange("k p t -> p k t"), writes=["cau"])
        P.dma("sp", win[:], G.winM.rearrange("k p t -> p k t"), writes=["win"])
        P.dma("sp", Esel[:], G.Esel_d, writes=["Esel"])
        P.dma("sp", ov[:], G.ov_d.rearrange("k p s -> p k s"), writes=["ov"])
        RS = RotState(banks)
        wseq = []
        for ci_ in range(NCH):
            for g_ in range(4):
                subs_ = [(2 * g_ + hf, "q", hf * 2) for hf in range(2)]
                for br_ in range(3):
                    subs_ += [(8 + (br_ * 4 + g_) * 2 + hf, "z", br_ * 4 + hf * 2) for hf in range(2)]
                wseq += [("qg", s_) for s_ in subs_]
            wseq += [("o", sb_) for sb_ in range(8)]
        wstate = {"issued": 0, "used": 0}

        def w_issue_upto(n):
            while wstate["issued"] < min(n, len(wseq)):
                i = wstate["issued"]
                kind_, info = wseq[i]
                src = wblk(Wq, info[0], 256) if kind_ == "qg" else wblk(G.wbf_o[jl], info, 256)
                P.dma("sp", wb[i % 4][:], src, reads=["wbf"], writes=[f"wb{i % 4}"])
                wstate["issued"] += 1

        def w_next():
            i = wstate["used"]
            w_issue_upto(i + 1)
            wstate["used"] += 1
            return wb[i % 4], f"wb{i % 4}", i

        w_issue_upto(4)
        for ci in range(NCH):
            t0 = ci * TC
            nkt = 4 * ci + 4
            sq = big2
            P.dma("sp", xc, xv[:, :, t0:t0 + TC], writes=["big1"])
            P.dma("sp", cmk[:], G.cmaskc[ci].rearrange("k p t -> p k t"), writes=["cmk"])
            P.dma("sp", selA[:], G.selA_d[:, ci * 4:(ci + 1) * 4, :], writes=["selA"])
            P.dma("sp", selB[:], G.selB_d[:, ci * 4:(ci + 1) * 4, :], writes=["selB"])
            act(P, sq[:], xc, AF.Square, ["big1"], ["big2"])
            for k in range(KT):
                mm(P, banks[7][:], G.ones_bf[:], sq[:, k, :], k == 0, k == KT - 1, ["ones_bf", "big2"], ["nb7"])
            act(P, rstd[:], banks[7][:], AF.Sqrt, ["nb7", "epsc"], ["rstd"], bias=G.epsc[:, 0:1], scale=1.0 / D)
            P.op("dve", lambda e, o=rstd[:], i=rstd[:]: e.reciprocal(out=o, in_=i), ["rstd"], ["rstd"])
            tt(P, "dve", xc, xc, rstd[:].unsqueeze(1).broadcast_to([128, KT, TC]), ALU.mult, ["big1", "rstd"], ["big1"])
            for k in range(KT):
                if k % 2 == 0:
                    act(P, hT[:, k, :], xc[:, k, :], AF.Identity, ["big1", "gsc", "modT"], ["hT"],
                        bias=G.modT[layer][:, k:k + 1], scale=G.gsc[layer][:, k:k + 1])
                else:
                    ts(P, "dve", hT[:, k, :], xc[:, k, :], G.gsc[layer][:, k:k + 1], ALU.mult, ["big1", "gsc", "modT"], ["hT"],
                       s2=G.modT[layer][:, k:k + 1], op1=ALU.add)
            P.dma("sp", wg[:], wblk(G.wbf_g[jl], 0, 48), reads=["wbf"], writes=["wg"])
            for k in range(KT):
                mm(P, banks[7][0:48, :], wg[:, k, :], hT[:, k, :], k == 0, k == KT - 1, ["wg", "hT"], ["nb7"])
            act(P, gT[:], banks[7][0:48, :], AF.Sigmoid, ["nb7"], ["gT"])
            P.dma("sp", G.g_scr[:, :], gT[:], reads=["gT"], writes=["g_scr"])
            oT = big2
            for g in range(4):
                kw0 = max(0, ci * 4 - 4)
                nkw = nkt - kw0
                P.dma("sp", kwT[:, 0:nkw * 128], G.kvT_scr[3, g][:, kw0 * 128:nkt * 128], reads=["kv_scr"], writes=["kwT"])
                P.dma("sp", vw[:, 0:nkw, :], G.vtok_scr[1, g].rearrange("p (a d) -> p a d", d=128)[:, kw0:nkt, :],
                      reads=["kv_scr"], writes=["vw"])
                P.dma("sp", ksT[:, 0:nkt * 128], G.kvT_scr[2, g][:, 0:nkt * 128], reads=["kv_scr"], writes=["ksT"])
                P.dma("sp", vs[:, 0:nkt, :], G.vtok_scr[0, g].rearrange("p (a d) -> p a d", d=128)[:, 0:nkt, :],
                      reads=["kv_scr"], writes=["vs"])
                subs = [(2 * g + hf, "q", hf * 2) for hf in range(2)]
                for br in range(3):
                    subs += [(8 + (br * 4 + g) * 2 + hf, "z", br * 4 + hf * 2) for hf in range(2)]
                for sidx, (blk, kind, j0) in enumerate(subs):
                    w, wk, wi = w_next()
                    for jj in range(2):
                        bsel = [7, 0, 1, 2][(sidx * 2 + jj) % 4]
                        b = banks[bsel]
                        bk = f"nb{bsel}"
                        for k in range(KT):
                            mm(P, b[:], w[:, k, jj * 128:(jj + 1) * 128], hT[:, k, :], k == 0, k == KT - 1, [wk, "hT"], [bk])
                        if kind == "q":
                            act(P, qT[:, j0 + jj, :], b[:], AF.Identity, [bk], ["qT"], scale=QSCALE)
                        else:
                            act(P, zT[:, j0 + jj, :], b[:], AF.Silu, [bk], ["zT"])
                    w_issue_upto(wi + 5)
                ntl = 1 if ci < 4 else 2

                def make_epilogue(kind, ob, okk, lb, lkk, gr0, gk0, pms, hh, h_glob):
                    def epi():
                        rv, rk = rinv[RS.ri % 2], f"rinv{RS.ri % 2}"
                        RS.ri += 1
                        act(P, rv[:], lb[:], AF.Ln, [lkk, "tinyc"], [rk], bias=tinyc[:, 0:1], scale=1.0)
                        act(P, rv[:], rv[:], AF.Exp, [rk], [rk], scale=-1.0)
                        if kind == "cmp":
                            for nt, (pm_, pmk) in enumerate(pms):
                                if hh == 0:
                                    tt(P, "pool", pc[:, nt, :], pm_[:], rv[:], ALU.mult, [pmk, rk], ["pc"])
                                else:
                                    tt(P, "pool", ptmp[:], pm_[:], rv[:], ALU.mult, [pmk, rk], ["ptmp"])
                                    tt(P, "pool", pc[:, nt, :], pc[:, nt, :], ptmp[:], ALU.add, ["pc", "ptmp"], ["pc"])
                            emit_combine(P, G, ob, okk, gr0, gk0, RS, o1, oacc, zT, rv, rk, 0, g, hh, first=True)
                            cp(P, "pool", oT[:, h_glob, :], oacc[:], ["oacc"], ["big2"])
                        elif kind == "win":
                            emit_combine(P, G, ob, okk, gr0, gk0, RS, o1, oacc, zT, rv, rk, 2, g, hh, first=True)
                            tt(P, "pool", oT[:, h_glob, :], oT[:, h_glob, :], oacc[:], ALU.add, ["big2", "oacc"], ["big2"])
                        else:
                            emit_combine(P, G, ob, okk, gr0, gk0, RS, o1, oacc, zT, rv, rk, 1, g, hh, first=True)
                            tt(P, "dve", oT[:, h_glob, :], oT[:, h_glob, :], oacc[:], ALU.add, ["big2", "oacc"], ["big2"])
                    return epi

                for hh in range(4):
                    h_glob = g * 4 + hh
                    tiles = [(G.kcT[:, g, nt * 128:(nt + 1) * 128], ["kcT"], cmk[:, nt, :], ["cmk"], G.vc[:, g, nt, :], ["vc"]) for nt in range(ntl)]
                    ob, okk, lb, lkk = RS.next_ol()
                    gr0, gk0 = emit_gate_dma(P, G, grep_, RS, 0, g, hh)
                    pms = attn_block(P, G, RS, banks, Pe, Pm, tiles, qT[:, hh, :], ob, okk, lb, lkk)
                    RS.pending = make_epilogue("cmp", ob, okk, lb, lkk, gr0, gk0, pms, hh, h_glob)

                def sel_chain(t4):
                    for nt in range(ntl):
                        mm(P, banks[7][:, t4 * 64:(t4 + 1) * 64], pc[:, nt, t4 * 128:(t4 + 1) * 128], ov[:, nt, :], nt == 0, nt == ntl - 1,
                           ["pc", "ov"], ["nb7"])
                    sc_, wk_, m8_, sf_ = score[:, t4, :], work[:, t4, :], m8[:, t4, :], self_[:, t4, :]
                    kk = [f"score{t4}", f"work{t4}", f"m8{t4}", f"self{t4}"]
                    tt(P, "dve", sc_, banks[7][:, t4 * 64:(t4 + 1) * 64], selA[:, t4, :], ALU.mult, ["nb7", "selA"], [kk[0]])
                    tt(P, "dve", sc_, sc_, selB[:, t4, :], ALU.add, [kk[0], "selB"], [kk[0]])
                    P.op("dve", lambda e, o=m8_[:, 0:8], i=sc_: e.max(out=o, in_=i), [kk[0]], [kk[2]])
                    P.op("dve", lambda e, o=wk_, a=m8_[:, 0:8], b=sc_: e.match_replace(out=o, in_to_replace=a, in_values=b, imm_value=-1e9),
                         [kk[0], kk[2]], [kk[1]])
                    P.op("dve", lambda e, o=m8_[:, 8:16], i=wk_: e.max(out=o, in_=i), [kk[1]], [kk[2]])
                    ts(P, "dve", wk_, sc_, m8_[:, 15:16], ALU.is_ge, [kk[0], kk[2]], [kk[1]])
                    stt(P, "dve", sf_, sc_, -500.0, wk_, ALU.is_gt, ALU.mult, [kk[0], kk[1]], [kk[3]])

                def sel_transpose(t4):
                    tr(P, banks[7][0:64, 256 + (t4 % 2) * 128:256 + (t4 % 2 + 1) * 128], self_[:, t4, :], G.ident[:], [f"self{t4}", "ident"], ["nb7"])
                    cp(P, "act", selTb[:, t4 * 128:(t4 + 1) * 128], banks[7][0:64, 256 + (t4 % 2) * 128:256 + (t4 % 2 + 1) * 128], ["nb7"], ["selTb"])

                for hh in range(4):
                    h_glob = g * 4 + hh
                    tiles = []
                    for a_ in range(nkw):
                        rel = (kw0 + a_) - (ci * 4 - 4)
                        tiles.append((kwT[:, a_ * 128:(a_ + 1) * 128], ["kwT"], win[:, rel, :], ["win"], vw[:, a_, :], ["vw"]))
                    ob, okk, lb, lkk = RS.next_ol()
                    gr0, gk0 = emit_gate_dma(P, G, grep_, RS, 2, g, hh)
                    attn_block(P, G, RS, banks, Pe, Pm, tiles, qT[:, hh, :], ob, okk, lb, lkk)
                    RS.pending = make_epilogue("win", ob, okk, lb, lkk, gr0, gk0, None, hh, h_glob)
                    if hh >= 1:
                        sel_transpose(hh - 1)
                    sel_chain(hh)
                sel_transpose(3)
                for kt in range(nkt):
                    sb_, sk = RS.next_s(banks)
                    mm(P, sb_[:], Esel[:, kt, :], selTb[:], True, True, ["Esel", "selTb"], [sk])
                    if kt >= 4 * ci:
                        tt(P, "dve", maskT[:, kt, :], sb_[:], cau[:, kt - 4 * ci, :], ALU.mult, [sk, "cau"], ["big1"])
                    else:
                        cp(P, "act" if kt % 2 else "dve", maskT[:, kt, :], sb_[:], [sk], ["big1"])
                for hh in range(4):
                    h_glob = g * 4 + hh
                    tiles = [(ksT[:, kt * 128:(kt + 1) * 128], ["ksT"], maskT[:, kt, :], ["big1"], vs[:, kt, :], ["vs"]) for kt in range(nkt)]
                    ob, okk, lb, lkk = RS.next_ol()
                    gr0, gk0 = emit_gate_dma(P, G, grep_, RS, 1, g, hh)
                    attn_block(P, G, RS, banks, Pe, Pm, tiles, qT[:, hh, :], ob, okk, lb, lkk)
                    RS.pending = make_epilogue("sel", ob, okk, lb, lkk, gr0, gk0, None, hh, h_glob)
                RS.flush()
            P.dma("sp", xc, xv[:, :, t0:t0 + TC], reads=["x_scr_prev"], writes=["big1"])
            for sb in range(8):
                w, wk, wi = w_next()
                for jj in range(2):
                    ct = sb * 2 + jj
                    bsel = [7, 0, 1, 2][ct % 4]
                    b = banks[bsel]
                    bk = f"nb{bsel}"
                    for k in range(KT):
                        mm(P, b[:], w[:, k, jj * 128:(jj + 1) * 128], oT[:, k, :], k == 0, k == KT - 1, [wk, "big2"], [bk])
                    stt(P, "dve", xc[:, ct, :], b[:], gate[:, ct:ct + 1], xc[:, ct, :], ALU.mult, ALU.add, [bk, "modT", "big1"], ["big1"])
                w_issue_upto(wi + 5)
            P.dma("pool", xo[:, :, t0:t0 + TC], xc, reads=["big1"], writes=["x_scr"])
        P.emit(st)


class RotState:
    def __init__(self, banks):
        self.si = 0
        self.pi = 0
        self.gi = 0
        self.ri = 0
        self.oi = 0
        self.banks = banks
        self.pending = None

    def flush(self):
        if self.pending is not None:
            f = self.pending
            self.pending = None
            f()

    def next_s(self, banks=None):
        i = [0, 1, 2, 7][self.si % 4]
        self.si += 1
        return self.banks[i], f"nb{i}"

    def next_p(self, Pe, Pm):
        i = self.pi % len(Pe)
        self.pi += 1
        return Pe[i], f"Pe{i}", Pm[i], f"Pm{i}", self.pi

    def next_ol(self):
        i = self.oi % 2
        self.oi += 1
        return self.banks[3 + i], f"nb{3 + i}", self.banks[5 + i], f"nb{5 + i}"


def attn_block(P, G, RS, banks, Pe, Pm, tiles, q_ap, ob, okk, lb, lkk, LA=3):
    n = len(tiles)
    slots = []
    for i in range(n + LA):
        if i < n:
            kT_ap, kk, mask_ap, mk, v_ap, vk = tiles[i]
            sb_, sk = RS.next_s()
            mm(P, sb_[:], kT_ap, q_ap, True, True, list(kk) + ["qT"], [sk])
            pe_, pek, pm_, pmk, cnt = RS.next_p(Pe, Pm)
            act(P, pe_[:], sb_[:], AF.Exp, [sk], [pek])
            tt(P, "dve", pm_[:], pe_[:], mask_ap, ALU.mult, [pek] + list(mk), [pmk])
            slots.append((pm_, pmk))
        if i == min(n, LA) - 1:
            RS.flush()
        j = i - LA
        if j >= 0:
            pm_, pmk = slots[j]
            v_ap, vk = tiles[j][4], tiles[j][5]
            mm(P, ob[:], v_ap, pm_[:], j == 0, j == n - 1, list(vk) + [pmk], [okk])
            mm(P, lb[:], G.ones_bf[:], pm_[:], j == 0, j == n - 1, ["ones_bf", pmk], [lkk])
    return slots


def emit_gate_dma(P, G, grep_, RS, br, g, hh):
    r = br * 16 + g * 4 + hh
    i = RS.gi % len(grep_)
    RS.gi += 1
    gr, gk = grep_[i], f"grep{i}"
    P.dma("pool", gr[:], G.g_scr[r].partition_broadcast(128), reads=["g_scr"], writes=[gk])
    return gr, gk


def emit_combine(P, G, ob, okk, gr, gk, RS, o1, oacc, zT, rv, rk, br, g, hh, first):
    o, ok = o1[RS.gi % 2], f"o1{RS.gi % 2}"
    tt(P, "dve", o[:], rv[:], gr[:], ALU.mult, [rk, gk], [ok])
    tt(P, "dve", o[:], ob[:], o[:], ALU.mult, [okk, ok], [ok])
    if first:
        tt(P, "pool", oacc[:], o[:], zT[:, br * 4 + hh, :], ALU.mult, [ok, "zT"], ["oacc"])
    else:
        tt(P, "pool", o[:], o[:], zT[:, br * 4 + hh, :], ALU.mult, [ok, "zT"], [ok])
        tt(P, "pool", oacc[:], oacc[:], o[:], ALU.add, ["oacc", ok], ["oacc"])


from concourse.bass_utils import run_bass_kernel_spmd
import ml_dtypes


def phase_final(nc, G, x_src):
    with ExitStack() as st:
        P = Prog(nc)
        xc = [sbt(nc, st, f"xc{i}", [128, KT, TC], F32) for i in range(2)]
        sq = sbt(nc, st, "sq", [128, KT, TC], BF16)
        rstd = sbt(nc, st, "rstd", [128, TC], F32)
        bank = pst(nc, st, "fb0", [128, 512])
        xv = x_src.rearrange("(k p) t -> p k t", p=128)
        yv = G.yT.rearrange("(k p) t -> p k t", p=128)
        gfin = G.vecs[:, 5, :]
        for ci in range(NCH):
            t0 = ci * TC
            x = xc[ci % 2]
            kx = f"xc{ci % 2}"
            P.dma("sp", x[:], xv[:, :, t0:t0 + TC], writes=[kx])
            emit_norm(P, G, x, sq, rstd, bank, "fb0", {"xc": kx, "sq": "sq", "rstd": "rstd"})
            for k in range(KT):
                if k % 2 == 0:
                    act(P, x[:, k, :], x[:, k, :], AF.Identity, [kx, "vecs"], [kx], scale=gfin[:, k:k + 1])
                else:
                    ts(P, "dve", x[:, k, :], x[:, k, :], gfin[:, k:k + 1], ALU.mult, [kx, "vecs"], [kx])
            P.dma("act", yv[:, :, t0:t0 + TC], x[:], reads=[kx], writes=["y_out"])
        P.emit(st)


def build(cfg):
    nc = bass.Bass("TRN2", target_bir_lowering=False)
    dbg = cfg.get("debug", ())

    def din(name, shape, dt=F32):
        return nc.dram_tensor(name, list(shape), dt, kind="ExternalInput").ap()

    def dscr(name, shape, dt):
        kind = "ExternalOutput" if name in dbg else "Internal"
        return nc.dram_tensor(name, list(shape), dt, kind=kind).ap()

    G = Ctx()
    G.xT = din("xT", [D, L])
    G.cT = din("cT", [128, 16])
    G.vecs_d = din("vecs", [128, 10, 16])
    G.ident_d = din("ident", [128, 128])
    G.bmask_d = din("bmask", [128, 128])
    G.mod_w = din("mod_w", [4, D, 3 * D])
    G.mod_b = din("mod_b", [4, 3 * D])
    G.kv_mod_w = din("kv_mod_w", [D, 2 * D])
    G.kv_mod_b = din("kv_mod_b", [1, 2 * D])
    G.ssm_w_in = din("ssm_w_in", [2, D, 2 * D])
    G.ssm_w_glu = din("ssm_w_glu", [2, D, D])
    G.ssm_w_out = din("ssm_w_out", [2, D, D])
    G.lamgn_d = din("lamgn", [2, 128, 3, 64])
    G.lamch_d = din("lamch", [2, 128, 3, 1024])
    G.bgn_d = din("bgn", [2, 128, 2, 64, 16])
    G.cch_d = din("cch", [2, 128, 2, 16, 64])
    G.w_kv = din("w_kv", [D, 3072])
    G.cmp_peT = din("cmp_peT", [128, 2, 32])
    G.cmp_b1T = din("cmp_b1T", [128, 2])
    G.cmp_b2T = din("cmp_b2T", [128, 2])
    G.cmp_b2row = din("cmp_b2row", [128, 128])
    G.cmp_w1 = din("cmp_w1", [2, 4096, 128])
    G.cmp_w2 = din("cmp_w2", [2, 128, 128])
    G.nsa_w_qg = din("nsa_w_qg", [2, D, 8240])
    G.nsa_w_o = din("nsa_w_o", [2, D, D])
    G.cmaskc = din("cmaskc", [8, 2, 128, 512], BF16)
    G.cauM = din("cauM", [4, 128, 512], BF16)
    G.winM = din("winM", [8, 128, 512], BF16)
    G.Esel_d = din("Esel", [64, 32, 128], BF16)
    G.ov_d = din("ov", [2, 128, 64])
    G.selA_d = din("selA", [128, 32, 64])
    G.selB_d = din("selB", [128, 32, 64])
    G.kvT_scr = dscr("kvT_scr", [4, 4, 128, L], BF16)
    G.vtok_scr = dscr("vtok_scr", [2, 4, 128, 32 * 128], BF16)
    G.g_scr = dscr("g_scr", [48, 512], F32)
    G.wbf_in = dscr("wbf_in", [2, 8, 128, 8192], BF16)
    G.wbf_glu = dscr("wbf_glu", [2, 4, 128, 8192], BF16)
    G.wbf_out = dscr("wbf_out", [2, 4, 128, 8192], BF16)
    G.wbf_kv = dscr("wbf_kv", [6, 128, 8192], BF16)
    G.wbf_qg = dscr("wbf_qg", [2, 32, 128, 4096], BF16)
    G.wbf_g = dscr("wbf_g", [2, 1, 128, 16 * 48], BF16)
    G.wbf_o = dscr("wbf_o", [2, 8, 128, 4096], BF16)
    G.uT_scr = dscr("uT_scr", [D, L], BF16)
    G.zT_scr = dscr("zT_scr", [D, L], BF16)
    G.ygT_scr = dscr("ygT_scr", [D, L], BF16)
    G.xres = dscr("xres", [D, L], F32)
    G.yT = nc.dram_tensor("yT", [D, L], F32, kind="ExternalOutput").ap()
    G.dbgB = None
    if "dbgB" in dbg:
        G.dbgB = nc.dram_tensor("dbgB", [128, 8192], F32, kind="ExternalOutput").ap()
        G.dbg_ct = cfg.get("dbg_ct", 1)

    with ExitStack() as gst:
        G.cact = sbt(nc, gst, "cact", [128, 16], F32)
        G.vecs = sbt(nc, gst, "vecs", [128, 10, 16], F32)
        G.ident = sbt(nc, gst, "ident", [128, 128], F32)
        G.bmask = sbt(nc, gst, "bmask", [128, 128], F32)
        G.ones_bf = sbt(nc, gst, "ones_bf", [128, 128], BF16)
        G.epsc = sbt(nc, gst, "epsc", [128, 1], F32)
        G.modT = [sbt(nc, gst, f"modT{l}", [128, 48], F32) for l in range(4)]
        G.kvmodT = sbt(nc, gst, "kvmodT", [128, 32], F32)
        G.gsc = [sbt(nc, gst, f"gsc{l}", [128, 16], F32) for l in range(4)]
        G.gkv = sbt(nc, gst, "gkv", [128, 16], F32)
        G.kcT = sbt(nc, gst, "kcT", [128, 4, 256], BF16)
        G.vc = sbt(nc, gst, "vc", [128, 4, 2, 128], BF16)
        phase_prologue(nc, G)
        if cfg.get("precast"):
            with ExitStack() as pst_:
                Pp = Prog(nc)
                if 1 in cfg["layers"] and 0 not in cfg["layers"]:
                    emit_cast_blocks(Pp, G.ssm_w_in[1], 0, 8, G.wbf_in[1])
                    emit_cast_blocks(Pp, G.ssm_w_glu[1], 0, 4, G.wbf_glu[1])
                    emit_cast_blocks(Pp, G.ssm_w_out[1], 0, 4, G.wbf_out[1])
                if 2 in cfg["layers"] and 1 not in cfg["layers"]:
                    emit_cast_blocks(Pp, G.w_kv, 0, 6, G.wbf_kv)
                    emit_cast_qg(Pp, G, 0)
                if 3 in cfg["layers"] and 2 not in cfg["layers"]:
                    emit_cast_blocks(Pp, G.w_kv, 0, 6, G.wbf_kv)
                    emit_cast_qg(Pp, G, 1)
                Pp.emit(pst_)
        x_cur = G.xT
        for layer in cfg["layers"]:
            if layer < 2:
                with ExitStack() as lst:
                    S = Ctx()
                    S.apow = sbt(nc, lst, "apow", [128, 2, 64, 17], F32)
                    S.hsA = sbt(nc, lst, "hsA", [128, 2, 64, 8], F32)
                    S.hsAn = sbt(nc, lst, "hsAn", [128, 64, 8], F32)
                    S.Bbar = sbt(nc, lst, "Bbar", [128, 2, 64, 16], F32)
                    S.a_ch = sbt(nc, lst, "a_ch", [128, 2, 1024], F32)
                    phase_s5_A(nc, G, layer, x_cur)
                    phase_s5_setup(nc, G, S, layer)
                    phase_s5_B(nc, G, S, layer)
                phase_s5_C(nc, G, layer, x_cur, G.xres)
                x_cur = G.xres
            else:
                if layer == 2 or not cfg.get("kv_done"):
                    cfg["kv_done"] = True
                    phase_kv(nc, G, x_cur)
                    phase_cmp(nc, G)
                if not cfg.get("kv_only"):
                    phase_nsa(nc, G, layer, x_cur)
                    x_cur = G.xres
        phase_final(nc, G, x_cur)
    return nc


def _vec16(v):
    return np.ascontiguousarray(np.asarray(v, np.float32).reshape(16, 128).T)


def host_common(inp):
    f = lambda a: np.ascontiguousarray(np.asarray(a, np.float32))
    m = {}
    vl = [inp["norm_g"][i] for i in range(4)] + [inp["kv_norm_g"], inp["final_norm_g"],
                                                 inp["ssm_d"][0], inp["ssm_d"][1], inp["ssm_b_glu"][0], inp["ssm_b_glu"][1]]
    m["vecs"] = np.ascontiguousarray(np.stack([_vec16(v) for v in vl], axis=1))
    m["ident"] = np.eye(128, dtype=np.float32)
    p = np.arange(128)
    m["bmask"] = (p[:, None] // 16 == p[None, :] // 16).astype(np.float32)
    for k in ["mod_w", "mod_b", "kv_mod_w", "ssm_w_in", "ssm_w_glu", "ssm_w_out"]:
        m[k] = f(inp[k])
    m["kv_mod_b"] = f(inp["kv_mod_b"]).reshape(1, -1)
    lamgn = np.zeros((2, 128, 3, 64), np.float32)
    lamch = np.zeros((2, 128, 3, 1024), np.float32)
    bgn = np.zeros((2, 128, 2, 64, 16), np.float32)
    cch = np.zeros((2, 128, 2, 16, 64), np.float32)
    for l in range(2):
        ls_full = np.broadcast_to(f(inp["ssm_log_step"][l])[:, None], (128, 64))
        for i, arr in enumerate([f(inp["ssm_lam_re"][l]), f(inp["ssm_lam_im"][l]), ls_full]):
            a5 = arr.reshape(16, 8, 4, 16)
            lamgn[l, :, i, :] = a5.transpose(1, 3, 0, 2).reshape(128, 64)
            a3 = arr.reshape(16, 8, 64).transpose(1, 0, 2)
            lamch[l, :, i, :] = np.broadcast_to(a3[:, None], (8, 16, 16, 64)).reshape(128, 1024)
        for i, arr in enumerate([f(inp["ssm_b_re"][l]), f(inp["ssm_b_im"][l])]):
            a6 = arr.reshape(16, 8, 4, 16, 16)
            bgn[l, :, i] = a6.transpose(1, 3, 0, 2, 4).reshape(128, 64, 16)
        for i, arr in enumerate([f(inp["ssm_c_re"][l]), f(inp["ssm_c_im"][l])]):
            a4 = arr.reshape(16, 8, 16, 64)
            cch[l, :, i] = a4.transpose(1, 2, 0, 3).reshape(128, 16, 64)
    m["lamgn"], m["lamch"], m["bgn"], m["cch"] = lamgn, lamch, bgn, cch
    for k in ["w_kv", "cmp_w1", "cmp_w2", "nsa_w_qg", "nsa_w_o"]:
        m[k] = f(inp[k])
    m["cmp_peT"] = np.ascontiguousarray(f(inp["cmp_pe"]).transpose(2, 0, 1))
    m["cmp_b1T"] = np.ascontiguousarray(f(inp["cmp_b1"]).T)
    m["cmp_b2T"] = np.ascontiguousarray(f(inp["cmp_b2"]).T)
    m["cmp_b2row"] = np.ascontiguousarray(np.broadcast_to(f(inp["cmp_b2"])[1][None, :], (128, 128)))
    bf = ml_dtypes.bfloat16
    kl = np.arange(128)[:, None]
    tl = np.arange(512)[None, :]
    cm = np.zeros((8, 2, 128, 512), np.float32)
    for ci in range(8):
        for nt in range(2):
            n = nt * 128 + kl
            cm[ci, nt] = ((16 * n + 31) <= (512 * ci + tl)) & (n < 255)
    m["cmaskc"] = cm.astype(bf)
    m["cauM"] = np.stack([((128 * kk + kl) <= tl) for kk in range(4)]).astype(np.float32).astype(bf)
    wm = []
    for rel in range(8):
        diff = tl - ((rel - 4) * 128 + kl)
        wm.append((diff >= 0) & (diff < 512))
    m["winM"] = np.stack(wm).astype(np.float32).astype(bf)
    s_ = np.arange(64)[:, None, None]
    kt_ = np.arange(32)[None, :, None]
    kk_ = np.arange(128)[None, None, :]
    m["Esel"] = (s_ == 2 * kt_ + kk_ // 64).astype(np.float32).astype(bf)
    c0 = np.arange(255)[:, None] * 16
    s0 = np.arange(64)[None, :] * 64
    ovm = (np.clip(np.minimum(c0 + 32, s0 + 64) - np.maximum(c0, s0), 0, None) / 16).astype(np.float32)
    ovp = np.zeros((256, 64), np.float32)
    ovp[:255] = ovm
    m["ov"] = np.ascontiguousarray(ovp.reshape(2, 128, 64))
    t = np.arange(4096)[:, None]
    blk = np.arange(64)[None, :]
    cur = t // 64
    valid = (blk <= cur)
    forced = ((blk == 0) | (blk == cur) | (blk == cur - 1))
    A = valid.astype(np.float32)
    B = (1000.0 * (forced & valid) - 1000.0 * (~valid)).astype(np.float32)
    m["selA"] = np.ascontiguousarray(A.reshape(32, 128, 64).transpose(1, 0, 2))
    m["selB"] = np.ascontiguousarray(B.reshape(32, 128, 64).transpose(1, 0, 2))
    return m


def host_percore(inp, b):
    return {"xT": np.ascontiguousarray(np.asarray(inp["x"][b], np.float32).T),
            "cT": _vec16(inp["c"][b])}


_NC_CACHE = {}


def kernel(**inputs):
    cfg = {"layers": [0, 1, 2, 3], "debug": ()}
    nc = build(cfg)
    common = host_common(inputs)
    in_maps = []
    for b in range(4):
        m = dict(common)
        m.update(host_percore(inputs, b))
        in_maps.append(m)
    res = run_bass_kernel_spmd(nc, in_maps, core_ids=list(range(4)))
    out = np.stack([np.ascontiguousarray(np.asarray(res.results[b]["yT"], np.float32).T) for b in range(4)], axis=0)
    return out
```

```python
import numpy as np
import concourse.bass as bass
import concourse.mybir as mybir

F32 = mybir.dt.float32
BF16 = mybir.dt.bfloat16
I32 = mybir.dt.int32
AF = mybir.ActivationFunctionType
ALU = mybir.AluOpType
AX = mybir.AxisListType

EPOCH = 20000
NDMA_SEM = 10


class Prog:
    ENGS = ("pe", "dve", "act", "pool", "sp")

    def __init__(self, nc):
        self.nc = nc
        self.ops = []
        self.ctx = []

    def op(self, eng, fn, reads=(), writes=(), dma=False):
        self.ops.append((eng, fn, tuple(reads), tuple(writes), dma))

    def dma(self, q, out, in_, reads=(), writes=(), **kw):
        self.op(q, lambda e: e.dma_start(out=out, in_=in_, **kw), reads, writes, dma=True)

    def emit(self, stack):
        nc = self.nc
        SS = SemState.get(nc)
        esems, dsems, dcount, drr, ccount, known, semobj = SS.esems, SS.dsems, SS.dcount, SS.drr, SS.ccount, SS.known, SS.semobj
        last_w = {}
        readers = {}
        streams = {e: [] for e in self.ENGS}

        def need(eng, tok, waits):
            if tok is None:
                return
            sid, val = tok
            if eng == "pe" and sid[0] == "c" and sid[1] == "pe":
                return
            if known[eng].get(sid, 0) >= val:
                return
            known[eng][sid] = val
            waits[sid] = max(waits.get(sid, 0), val)

        for (eng, fn, r, w, dma) in self.ops:
            waits = {}
            for k in r:
                need(eng, last_w.get(k), waits)
            for k in w:
                need(eng, last_w.get(k), waits)
                for t in readers.get(k, ()):
                    need(eng, t, waits)
            if dma:
                i = drr[eng]
                drr[eng] = (i + 1) % NDMA_SEM
                s = dsems[eng][i]
                sid = ("d", eng, i)
                semobj[sid] = s
                if dcount[eng][i] > 0:
                    need(eng, (sid, dcount[eng][i]), waits)
                dcount[eng][i] += 16
                tok = (sid, dcount[eng][i])
                inc = (s, 16)
            else:
                c = ccount[eng]
                ep = c // EPOCH
                s = esems[eng][ep]
                sid = ("c", eng, ep)
                semobj[sid] = s
                ccount[eng] = c + 1
                tok = (sid, (c % EPOCH) + 1)
                inc = (s, 1)
            streams[eng].append((waits, fn, inc))
            for k in w:
                last_w[k] = tok
                readers[k] = []
            for k in r:
                readers.setdefault(k, []).append(tok)

        final_waits = {}
        for e in dsems:
            for i in range(NDMA_SEM):
                if dcount[e][i] > 0:
                    final_waits[("d", e, i)] = dcount[e][i]
        for e in self.ENGS:
            c = ccount[e]
            if c > 0:
                ep = (c - 1) // EPOCH
                final_waits[("c", e, ep)] = ((c - 1) % EPOCH) + 1
        for e in self.ENGS:
            for sid, val in final_waits.items():
                known[e][sid] = max(known[e].get(sid, 0), val)

        block = stack.enter_context(nc.Block())

        def make(engname):
            def body(engine):
                for (waits, fn, inc) in streams[engname]:
                    for sid, val in waits.items():
                        engine.wait_ge(semobj[sid], val)
                    ins = fn(engine)
                    ins.then_inc(inc[0], inc[1])
                for sid, val in final_waits.items():
                    engine.wait_ge(semobj[sid], val)
            return body

        block.tensor(make("pe"))
        block.vector(make("dve"))
        block.scalar(make("act"))
        block.gpsimd(make("pool"))
        block.sync(make("sp"))
        self.stats = {e: len(streams[e]) for e in self.ENGS}


class SemState:
    _inst = {}
    NEPOCH = {"pe": 14, "dve": 6, "act": 6, "pool": 5, "sp": 1}

    @classmethod
    def get(cls, nc):
        if id(nc) not in cls._inst:
            cls._inst.clear()
            cls._inst[id(nc)] = cls(nc)
        return cls._inst[id(nc)]

    def __init__(self, nc):
        from contextlib import ExitStack
        self.stack = ExitStack()
        st = self.stack
        self.esems = {e: [st.enter_context(nc.semaphore(f"s_{e}_{i}")) for i in range(n)] for e, n in self.NEPOCH.items()}
        self.dsems = {e: [st.enter_context(nc.semaphore(f"d_{e}_{i}")) for i in range(NDMA_SEM)] for e in ("sp", "act", "pool")}
        self.dcount = {e: [0] * NDMA_SEM for e in self.dsems}
        self.drr = {e: 0 for e in self.dsems}
        self.ccount = {e: 0 for e in Prog.ENGS}
        self.known = {e: {} for e in Prog.ENGS}
        self.semobj = {}


from contextlib import ExitStack
import math

D = 2048
L = 4096
NCH = 8
TC = 512
KT = 16
EPS = 1e-6
TWO_PI = 6.283185


def mm(P, out, lhsT, rhs, start, stop, r, w):
    P.op("pe", lambda e, o=out, l=lhsT, rr=rhs, s=start, t=stop: e.matmul(o, l, rr, start=s, stop=t), r, w)


def tr(P, out, in_, ident, r, w):
    P.op("pe", lambda e, o=out, i=in_, d=ident: e.transpose(o, i, d), r, w)


def act(P, out, in_, func, r, w, bias=None, scale=None):
    kw = {}
    if bias is not None:
        kw["bias"] = bias
    if scale is not None:
        kw["scale"] = scale
    P.op("act", lambda e, o=out, i=in_, f=func, k=kw: e.activation(out=o, in_=i, func=f, **k), r, w)


def tt(P, eng, out, in0, in1, op, r, w):
    P.op(eng, lambda e, o=out, a=in0, b=in1, p=op: e.tensor_tensor(out=o, in0=a, in1=b, op=p), r, w)


def ts(P, eng, out, in0, s1, op0, r, w, s2=None, op1=None):
    if op1 is None:
        P.op(eng, lambda e, o=out, a=in0, x=s1, p=op0: e.tensor_scalar(out=o, in0=a, scalar1=x, scalar2=None, op0=p), r, w)
    else:
        P.op(eng, lambda e, o=out, a=in0, x=s1, y=s2, p=op0, q=op1: e.tensor_scalar(out=o, in0=a, scalar1=x, scalar2=y, op0=p, op1=q), r, w)


def stt(P, eng, out, in0, scalar, in1, op0, op1, r, w):
    P.op(eng, lambda e, o=out, a=in0, s=scalar, b=in1, p=op0, q=op1: e.scalar_tensor_tensor(out=o, in0=a, scalar=s, in1=b, op0=p, op1=q), r, w)


def cp(P, eng, out, in_, r, w):
    if eng == "act":
        P.op(eng, lambda e, o=out, i=in_: e.copy(out=o, in_=i), r, w)
    else:
        P.op(eng, lambda e, o=out, i=in_: e.tensor_copy(out=o, in_=i), r, w)


def memset(P, eng, out, val, w):
    P.op(eng, lambda e, o=out, v=val: e.memset(o, v), (), w)


class Ctx:
    pass


_UID = [0]


def sbt(nc, st, name, shape, dt):
    _UID[0] += 1
    return st.enter_context(nc.sbuf_tensor(f"s{_UID[0]}_{name}", list(shape), dt))


def pst(nc, st, name, shape, dt=None):
    _UID[0] += 1
    return st.enter_context(nc.psum_tensor(f"p{_UID[0]}_{name}", list(shape), dt or F32))


def phase_prologue(nc, G):
    with ExitStack() as st:
        P = Prog(nc)
        W = [sbt(nc, st, f"pw{i}", [128, 3072], F32) for i in range(3)]
        row = sbt(nc, st, "prow", [1, 6144], F32)
        brow = sbt(nc, st, "pbrow", [1, 6144], F32)
        one11 = sbt(nc, st, "one11", [1, 1], F32)
        banks = [pst(nc, st, f"pb{i}", [128, 512]) for i in range(8)]
        memset(P, "dve", one11[:], 1.0, ["one11"])
        P.dma("sp", G.cact[:], G.cT, writes=["cact"])
        act(P, G.cact[:], G.cact[:], AF.Silu, ["cact"], ["cact"])
        P.dma("sp", G.vecs[:], G.vecs_d, writes=["vecs"])
        P.dma("sp", G.ident[:], G.ident_d, writes=["ident"])
        P.dma("sp", G.bmask[:], G.bmask_d, writes=["bmask"])
        memset(P, "dve", G.ones_bf[:], 1.0, ["ones_bf"])
        memset(P, "dve", G.epsc[:], EPS, ["epsc"])
        emit_cast_blocks(P, G.ssm_w_in[0], 0, 8, G.wbf_in[0])
        emit_cast_blocks(P, G.ssm_w_glu[0], 0, 4, G.wbf_glu[0])
        emit_cast_blocks(P, G.ssm_w_out[0], 0, 4, G.wbf_out[0])
        slot = 0
        jobs = [(G.mod_w[l], G.mod_b[l:l + 1, :], 6144, G.modT[l]) for l in range(4)]
        jobs.append((G.kv_mod_w, G.kv_mod_b, 4096, G.kvmodT))
        for (Wd, bd, ncols, outT) in jobs:
            half = ncols // 2
            nb = half // 512
            P.dma("act", brow[0:1, 0:ncols], bd, writes=["brow"])
            for hh in range(2):
                for k in range(KT):
                    wt = W[slot % 3]
                    wk = f"pw{slot % 3}"
                    slot += 1
                    P.dma("sp" if k % 2 == 0 else "act", wt[:, 0:half], Wd[k * 128:(k + 1) * 128, hh * half:(hh + 1) * half], writes=[wk])
                    for n in range(nb):
                        mm(P, banks[n][0:1, :], G.cact[:, k:k + 1], wt[:, n * 512:(n + 1) * 512], k == 0, k == KT - 1,
                           ["cact", wk], [f"pb{n}"])
                for n in range(nb):
                    c0 = hh * half + n * 512
                    tt(P, "dve", row[0:1, c0:c0 + 512], banks[n][0:1, :], brow[0:1, c0:c0 + 512], ALU.add,
                       [f"pb{n}", "brow"], ["row"])
            nt = ncols // 128
            for j in range(nt):
                mm(P, banks[7][:, j:j + 1], row[0:1, j * 128:(j + 1) * 128], one11[0:1, 0:1], True, True,
                   ["row", "one11"], ["pb7"])
            cp(P, "dve", outT[:, 0:nt], banks[7][:, 0:nt], ["pb7"], ["modT"])
        for l in range(4):
            stt(P, "dve", G.gsc[l][:], G.modT[l][:, 16:32], 1.0, G.vecs[:, l, :], ALU.add, ALU.mult, ["modT", "vecs"], ["gsc"])
        stt(P, "dve", G.gkv[:], G.kvmodT[:, 16:32], 1.0, G.vecs[:, 4, :], ALU.add, ALU.mult, ["modT", "vecs"], ["gsc"])
        P.emit(st)


def emit_norm(P, G, xc, sq, rstd, ps_bank, ps_key, keys):
    kx, ksq, krs = keys["xc"], keys["sq"], keys["rstd"]
    act(P, sq[:], xc[:], AF.Square, [kx], [ksq])
    for k in range(KT):
        mm(P, ps_bank[:], G.ones_bf[:], sq[:, k, :], k == 0, k == KT - 1, ["ones_bf", ksq], [ps_key])
    act(P, rstd[:], ps_bank[:], AF.Sqrt, [ps_key, "epsc"], [krs], bias=G.epsc[:, 0:1], scale=1.0 / D)
    P.op("dve", lambda e, o=rstd[:], i=rstd[:]: e.reciprocal(out=o, in_=i), [krs], [krs])
    tt(P, "dve", xc[:], xc[:], rstd[:].unsqueeze(1).broadcast_to([128, KT, TC]), ALU.mult, [kx, krs], [kx])


def emit_affine(P, hT, xc, gsc, shift, kx, kh):
    for k in range(KT):
        eng = "act" if k % 2 == 0 else "dve"
        if eng == "act":
            act(P, hT[:, k, :], xc[:, k, :], AF.Identity, [kx, "gsc", "modT"], [kh], bias=shift[:, k:k + 1], scale=gsc[:, k:k + 1])
        else:
            ts(P, "dve", hT[:, k, :], xc[:, k, :], gsc[:, k:k + 1], ALU.mult, [kx, "gsc", "modT"], [kh], s2=shift[:, k:k + 1], op1=ALU.add)


def emit_cast(P, src, dst, rows=256, cols=2048):
    R, N = src.shape
    for r0 in range(0, R, rows):
        for c0 in range(0, N, cols):
            cw = min(cols, N - c0)
            P.dma("pool", dst[r0:r0 + rows, c0:c0 + cw], src[r0:r0 + rows, c0:c0 + cw], writes=["wbf_next"])


def emit_cast_blocks(P, src, c0, nblk, dst, b0=0, cw=512):
    sv = src.rearrange("(k p) c -> p k c", p=128)
    for b in range(nblk):
        P.dma("pool", dst[b0 + b].rearrange("p (k c) -> p k c", c=cw), sv[:, :, c0 + b * cw:c0 + (b + 1) * cw], writes=["wbf_next"])


def wblk(dst, b, cw=512):
    return dst[b].rearrange("p (k c) -> p k c", c=cw)


def wview(Wd, c0, cw):
    return Wd.rearrange("(k p) c -> p k c", p=128)[:, :, c0:c0 + cw]


def phase_s5_A(nc, G, l, x_src):
    with ExitStack() as st:
        P = Prog(nc)
        xc = sbt(nc, st, "xc", [128, KT, TC], F32)
        sq = sbt(nc, st, "sq", [128, KT, TC], BF16)
        hTs = [sbt(nc, st, f"hT{i}", [128, KT, TC], BF16) for i in range(2)]
        rstd = sbt(nc, st, "rstd", [128, TC], F32)
        wb = [sbt(nc, st, f"wb{i}", [128, KT, 512], BF16) for i in range(3)]
        og = [sbt(nc, st, f"og{i}", [128, 4, TC], BF16) for i in range(2)]
        banks = [pst(nc, st, f"ab{i}", [128, 512]) for i in range(8)]
        xv = x_src.rearrange("(k p) t -> p k t", p=128)
        uv = G.uT_scr.rearrange("(k p) t -> p k t", p=128)
        zv = G.zT_scr.rearrange("(k p) t -> p k t", p=128)
        Wd = G.wbf_in[l]
        ws = 0
        oi = 0
        bi = 0
        def norm_affine(cj):
            P.dma("sp", xc[:], xv[:, :, cj * TC:(cj + 1) * TC], writes=["xc"])
            emit_norm(P, G, xc, sq, rstd, banks[7], "ab7", {"xc": "xc", "sq": "sq", "rstd": "rstd"})
            emit_affine(P, hTs[cj % 2], xc, G.gsc[l], G.modT[l][:, 0:16], "xc", f"hT{cj % 2}")

        norm_affine(0)
        for ci in range(NCH):
            t0 = ci * TC
            hT = hTs[ci % 2]
            hk = f"hT{ci % 2}"
            for cg in range(8):
                if cg == 4 and ci + 1 < NCH:
                    norm_affine(ci + 1)
                w = wb[ws % 3]
                wk = f"wb{ws % 3}"
                ws += 1
                P.dma("sp", w[:], wblk(Wd, cg), reads=["wbf"], writes=[wk])
                o = og[oi % 2]
                ok = f"og{oi % 2}"
                oi += 1
                for j in range(4):
                    b = banks[bi % 6]
                    bk = f"ab{bi % 6}"
                    bi += 1
                    for k in range(KT):
                        mm(P, b[:], w[:, k, j * 128:(j + 1) * 128], hT[:, k, :], k == 0, k == KT - 1, [wk, hk], [bk])
                    if cg < 4:
                        cp(P, "dve" if j % 2 else "act", o[:, j, :], b[:], [bk], [ok])
                    else:
                        act(P, o[:, j, :], b[:], AF.Silu, [bk], [ok])
                dst = uv if cg < 4 else zv
                cgl = cg % 4
                P.dma("pool", dst[:, cgl * 4:(cgl + 1) * 4, t0:t0 + TC], o[:], reads=[ok], writes=["uz_scr"])
        P.emit(st)


def phase_s5_C(nc, G, l, x_src, x_dst):
    with ExitStack() as st:
        P = Prog(nc)
        xcs = [sbt(nc, st, f"xc{i}", [128, KT, TC], F32) for i in range(2)]
        ygs = [sbt(nc, st, f"yg{i}", [128, KT, TC], BF16) for i in range(2)]
        zss = [sbt(nc, st, f"zs{i}", [128, KT, TC], BF16) for i in range(2)]
        mT = sbt(nc, st, "mT", [128, KT, TC], BF16)
        sg = [sbt(nc, st, f"sg{i}", [128, TC], F32) for i in range(2)]
        wb = [sbt(nc, st, f"wb{i}", [128, KT, 512], BF16) for i in range(3)]
        banks = [pst(nc, st, f"cb{i}", [128, 512]) for i in range(8)]
        xv = x_src.rearrange("(k p) t -> p k t", p=128)
        xo = x_dst.rearrange("(k p) t -> p k t", p=128)
        yv = G.ygT_scr.rearrange("(k p) t -> p k t", p=128)
        zv = G.zT_scr.rearrange("(k p) t -> p k t", p=128)
        bglu = G.vecs[:, 8 + l, :]
        gate = G.modT[l][:, 32:48]
        ws = 0
        bi = 0
        si = 0
        for ci in range(NCH):
            t0 = ci * TC
            xc, yg, zs = xcs[ci % 2], ygs[ci % 2], zss[ci % 2]
            kxc, kyg, kzs = f"xc{ci % 2}", f"yg{ci % 2}", f"zs{ci % 2}"
            P.dma("sp", yg[:], yv[:, :, t0:t0 + TC], reads=["yg_scr"], writes=[kyg])
            P.dma("sp", zs[:], zv[:, :, t0:t0 + TC], reads=["uz_scr"], writes=[kzs])
            P.dma("sp", xc[:], xv[:, :, t0:t0 + TC], writes=[kxc])
            for cg in range(4):
                w = wb[ws % 3]
                wk = f"wb{ws % 3}"
                ws += 1
                P.dma("sp", w[:], wblk(G.wbf_glu[l], cg), reads=["wbf"], writes=[wk])
                for j in range(4):
                    ct = cg * 4 + j
                    b = banks[bi % 8]
                    bk = f"cb{bi % 8}"
                    bi += 1
                    for k in range(KT):
                        mm(P, b[:], w[:, k, j * 128:(j + 1) * 128], yg[:, k, :], k == 0, k == KT - 1, [wk, kyg], [bk])
                    s = sg[si % 2]
                    sk = f"sg{si % 2}"
                    si += 1
                    act(P, s[:], b[:], AF.Sigmoid, [bk, "vecs"], [sk], bias=bglu[:, ct:ct + 1], scale=1.0)
                    tt(P, "dve", s[:], s[:], yg[:, ct, :], ALU.mult, [sk, kyg], [sk])
                    tt(P, "pool", mT[:, ct, :], s[:], zs[:, ct, :], ALU.mult, [sk, kzs], ["mT"])
            for cg in range(4):
                w = wb[ws % 3]
                wk = f"wb{ws % 3}"
                ws += 1
                P.dma("sp", w[:], wblk(G.wbf_out[l], cg), reads=["wbf"], writes=[wk])
                for j in range(4):
                    ct = cg * 4 + j
                    b = banks[bi % 8]
                    bk = f"cb{bi % 8}"
                    bi += 1
                    for k in range(KT):
                        mm(P, b[:], w[:, k, j * 128:(j + 1) * 128], mT[:, k, :], k == 0, k == KT - 1, [wk, "mT"], [bk])
                    stt(P, "dve", xc[:, ct, :], b[:], gate[:, ct:ct + 1], xc[:, ct, :], ALU.mult, ALU.add, [bk, "modT", kxc], [kxc])
            P.dma("pool", xo[:, :, t0:t0 + TC], xc[:], reads=[kxc], writes=["x_scr"])
        P.emit(st)


T1 = 16
NJ = L // T1
NLEV = 8


def emit_disc(P, nc, st, tag, F, lam, a_out, coef_out=None):
    kl, ka, kc = "lam" + tag, "a" + tag, "coef" + tag
    tm = [sbt(nc, st, f"dt{tag}{i}", [128, F], F32) for i in range(6)]
    ti = sbt(nc, st, f"di{tag}", [128, F], I32)
    K = [f"dt{tag}{i}" for i in range(6)]
    lr, li, ls = lam[:, 0, :], lam[:, 1, :], lam[:, 2, :]
    dt, lrdt, y, r, mag, sc = [t[:] for t in tm]
    act(P, dt, ls, AF.Exp, [kl], [K[0]])
    tt(P, "dve", lrdt, lr, dt, ALU.mult, [kl, K[0]], [K[1]])
    act(P, mag, lrdt, AF.Exp, [K[1]], [K[4]])
    tt(P, "dve", y, li, dt, ALU.mult, [kl, K[0]], [K[2]])
    ts(P, "dve", y, y, 1.0 / (2.0 * math.pi), ALU.mult, [K[2]], [K[2]])
    for which in range(2):
        if which == 1:
            ts(P, "dve", y, y, 0.25, ALU.add, [K[2]], [K[2]])
        cp(P, "dve", ti[:], y, [K[2]], ["di" + tag])
        cp(P, "dve", r, ti[:], ["di" + tag], [K[3]])
        tt(P, "dve", r, y, r, ALU.subtract, [K[2], K[3]], [K[3]])
        act(P, sc, r, AF.Sin, [K[3]], [K[5]], scale=TWO_PI)
        tt(P, "dve", a_out[:, 1 - which, :], mag, sc, ALU.mult, [K[4], K[5]], [ka])
    if coef_out is not None:
        den, am1, t1, t2 = dt, lrdt, y, r
        tt(P, "dve", den, lr, lr, ALU.mult, [kl], [K[0]])
        tt(P, "dve", t1, li, li, ALU.mult, [kl], [K[2]])
        tt(P, "dve", den, den, t1, ALU.add, [K[0], K[2]], [K[0]])
        P.op("dve", lambda e, o=den, i=den: e.reciprocal(out=o, in_=i), [K[0]], [K[0]])
        ts(P, "dve", am1, a_out[:, 0, :], -1.0, ALU.add, [ka], [K[1]])
        tt(P, "dve", t1, am1, lr, ALU.mult, [K[1], kl], [K[2]])
        tt(P, "dve", t2, a_out[:, 1, :], li, ALU.mult, [ka, kl], [K[3]])
        tt(P, "dve", t1, t1, t2, ALU.add, [K[2], K[3]], [K[2]])
        tt(P, "dve", coef_out[:, 0, :], t1, den, ALU.mult, [K[2], K[0]], [kc])
        tt(P, "dve", t1, a_out[:, 1, :], lr, ALU.mult, [ka, kl], [K[2]])
        tt(P, "dve", t2, am1, li, ALU.mult, [K[1], kl], [K[3]])
        tt(P, "dve", t1, t1, t2, ALU.subtract, [K[2], K[3]], [K[2]])
        tt(P, "dve", coef_out[:, 1, :], t1, den, ALU.mult, [K[2], K[0]], [kc])


def cmul(P, eng, o_re, o_im, x_re, x_im, s_re, s_im, t1, t2, r, w, tk):
    tt(P, eng, t1, x_im, s_im, ALU.mult, r, [tk + "1"])
    tt(P, eng, t2, x_im, s_re, ALU.mult, r, [tk + "2"])
    tt(P, eng, o_re, x_re, s_re, ALU.mult, r, w)
    tt(P, eng, o_im, x_re, s_im, ALU.mult, r, w)
    tt(P, eng, o_re, o_re, t1, ALU.subtract, list(w) + [tk + "1"], w)
    tt(P, eng, o_im, o_im, t2, ALU.add, list(w) + [tk + "2"], w)


def emit_pow17(P, pw, shape_mid, key, t1, t2, tk):
    raise NotImplementedError


def phase_s5_setup(nc, G, S, l):
    with ExitStack() as st:
        P = Prog(nc)
        lamgn = sbt(nc, st, "lamgn", [128, 3, 64], F32)
        lamch = sbt(nc, st, "lamch", [128, 3, 1024], F32)
        bgn = sbt(nc, st, "bgn", [128, 2, 64, 16], F32)
        a_gn = sbt(nc, st, "a_gn", [128, 2, 64], F32)
        coef = sbt(nc, st, "coefgn", [128, 2, 64], F32)
        tA = sbt(nc, st, "tA", [128, 64, 16], F32)
        tB = sbt(nc, st, "tB", [128, 64, 16], F32)
        P.dma("sp", lamgn[:], G.lamgn_d[l], writes=["lamgn"])
        P.dma("sp", lamch[:], G.lamch_d[l], writes=["lamch"])
        P.dma("act", bgn[:], G.bgn_d[l], writes=["bgn"])
        emit_disc(P, nc, st, "gn", 64, lamgn, a_gn, coef)
        emit_disc(P, nc, st, "ch", 1024, lamch, S.a_ch, None)
        cre = coef[:, 0, :].unsqueeze(2).broadcast_to([128, 64, 16])
        cim = coef[:, 1, :].unsqueeze(2).broadcast_to([128, 64, 16])
        cmul(P, "dve", S.Bbar[:, 0], S.Bbar[:, 1], bgn[:, 0], bgn[:, 1], cre, cim, tA[:], tB[:],
             ["bgn", "coefgn"], ["Bbar"], "tAB")
        ap_ = S.apow
        memset(P, "dve", ap_[:, 0, :, 0:1], 1.0, ["apow"])
        memset(P, "dve", ap_[:, 1, :, 0:1], 0.0, ["apow"])
        cp(P, "dve", ap_[:, 0, :, 1:2], a_gn[:, 0, :].unsqueeze(2), ["agn", "apow"], ["apow"])
        cp(P, "dve", ap_[:, 1, :, 1:2], a_gn[:, 1, :].unsqueeze(2), ["agn", "apow"], ["apow"])
        m = 1
        while m < 16:
            sre = ap_[:, 0, :, m:m + 1].broadcast_to([128, 64, m])
            sim = ap_[:, 1, :, m:m + 1].broadcast_to([128, 64, m])
            cmul(P, "dve", ap_[:, 0, :, m + 1:2 * m + 1], ap_[:, 1, :, m + 1:2 * m + 1],
                 ap_[:, 0, :, 1:m + 1], ap_[:, 1, :, 1:m + 1], sre, sim,
                 tA[:, :, 0:m], tB[:, :, 0:m], ["apow"], ["apow"], "tAB")
            m *= 2
        hs = S.hsA
        cp(P, "dve", hs[:, 0, :, 0:1], ap_[:, 0, :, 16:17], ["apow"], ["hsA"])
        cp(P, "dve", hs[:, 1, :, 0:1], ap_[:, 1, :, 16:17], ["apow"], ["hsA"])
        for lev in range(1, NLEV):
            cmul(P, "dve", hs[:, 0, :, lev:lev + 1], hs[:, 1, :, lev:lev + 1],
                 hs[:, 0, :, lev - 1:lev], hs[:, 1, :, lev - 1:lev], hs[:, 0, :, lev - 1:lev], hs[:, 1, :, lev - 1:lev],
                 tA[:, :, 0:1], tB[:, :, 0:1], ["hsA"], ["hsA"], "tAB")
        ts(P, "dve", S.hsAn[:], hs[:, 1], -1.0, ALU.mult, ["hsA"], ["hsA"])
        P.emit(st)


def phase_s5_B(nc, G, S, l):
    with ExitStack() as st:
        P = Prog(nc)
        cch = sbt(nc, st, "cch", [128, 2, 64], F32)
        apc = sbt(nc, st, "apc", [128, 2, 17, 64], F32)
        E = sbt(nc, st, "E", [128, 17, 2, 64], F32)
        Wt = sbt(nc, st, "Wt", [128, 16, 4, 2, 16], F32)
        tA = sbt(nc, st, "tA", [128, 17 * 64], F32)
        tB = sbt(nc, st, "tB", [128, 17 * 64], F32)
        Xm = [sbt(nc, st, f"Xm{i}", [128, 8, 128], F32) for i in range(2)]
        QW0 = sbt(nc, st, "QW0", [128, 8, 128], BF16)
        Bm = sbt(nc, st, "Bm", [128, 8, 128], BF16)
        PWs = sbt(nc, st, "PWs", [128, 4, 16, 2, 128], BF16)
        QWs = sbt(nc, st, "QWs", [128, 4, 16, 2, 128], BF16)
        KW = sbt(nc, st, "KW", [128, 16, 128], BF16)
        ygT = [sbt(nc, st, f"ygT{i}", [128, L], BF16) for i in range(1)]
        Z = [sbt(nc, st, f"Z{i}", [128, 4, 2, 128 + NJ], F32) for i in range(2)]
        H1 = sbt(nc, st, "H1", [128, 4, 2, NJ], F32)
        Xbf = sbt(nc, st, "Xbf", [128, 4, 2, NJ], BF16)
        uS = sbt(nc, st, "uS", [128, T1, NJ], BF16)
        banks = [pst(nc, st, f"bb{i}", [128, 512]) for i in range(8)]
        dcol = G.vecs[:, 6 + l, :]
        memset(P, "dve", Xbf[:, :, :, 0:1], 0.0, ["Xbf"])
        for zi_ in range(2):
            memset(P, "dve", Z[zi_][:, :, :, 0:128], 0.0, [f"Z{zi_}_{q}" for q in range(4)])
        if l == 0:
            emit_cast_blocks(P, G.ssm_w_in[1], 0, 8, G.wbf_in[1])
            emit_cast_blocks(P, G.ssm_w_glu[1], 0, 4, G.wbf_glu[1])
            emit_cast_blocks(P, G.ssm_w_out[1], 0, 4, G.wbf_out[1])
        else:
            emit_cast_blocks(P, G.w_kv, 0, 6, G.wbf_kv)
            emit_cast_qg(P, G, 0)
        uv = G.uT_scr.rearrange("(k p) t -> p k t", p=128)
        yv = G.ygT_scr.rearrange("(k p) t -> p k t", p=128)
        bmask3 = G.bmask[:].rearrange("p (a b) -> p a b", b=16)
        xi = 0
        for ct in range(KT):
            u = ygT[0]
            uk = "ygT0"
            P.dma("sp", cch[:], G.cch_d[l][:, :, ct, :], writes=["cch"])
            memset(P, "pool", apc[:, 0, 0:1, :], 1.0, ["apc"])
            memset(P, "pool", apc[:, 1, 0:1, :], 0.0, ["apc"])
            cp(P, "pool", apc[:, 0, 1:2, :], S.a_ch[:, 0, ct * 64:(ct + 1) * 64].unsqueeze(1), ["ach", "apc"], ["apc"])
            cp(P, "pool", apc[:, 1, 1:2, :], S.a_ch[:, 1, ct * 64:(ct + 1) * 64].unsqueeze(1), ["ach", "apc"], ["apc"])
            m = 1
            tA3 = tA[:].rearrange("p (k n) -> p k n", n=64)
            tB3 = tB[:].rearrange("p (k n) -> p k n", n=64)
            while m < 16:
                sre = apc[:, 0, m:m + 1, :].broadcast_to([128, m, 64])
                sim = apc[:, 1, m:m + 1, :].broadcast_to([128, m, 64])
                cmul(P, "pool", apc[:, 0, m + 1:2 * m + 1, :], apc[:, 1, m + 1:2 * m + 1, :],
                     apc[:, 0, 1:m + 1, :], apc[:, 1, 1:m + 1, :], sre, sim,
                     tA3[:, 0:m, :], tB3[:, 0:m, :], ["apc"], ["apc"], "tAB")
                m *= 2
            cre = cch[:, 0, :].unsqueeze(1).broadcast_to([128, 17, 64])
            cim = cch[:, 1, :].unsqueeze(1).broadcast_to([128, 17, 64])
            cmul(P, "pool", E[:, :, 0, :], E[:, :, 1, :], apc[:, 0], apc[:, 1], cre, cim, tA3, tB3,
                 ["apc", "cch"], ["E"], "tAB")
            ts(P, "pool", E[:, :, 1, :], E[:, :, 1, :], -1.0, ALU.mult, ["E"], ["E"])
            P.dma("sp", u[:], uv[:, ct, :], reads=["uz_scr"], writes=[uk])
            cp(P, "pool", uS[:], u[:].rearrange("p (j s) -> p s j", s=T1), [uk], ["uS"])
            for ri_o in range(2):
                pass
            T0 = ct * 4
            akre = S.apow[:, 0, T0:T0 + 4, 0:16].rearrange("p t k -> p k t").unsqueeze(3).broadcast_to([128, 16, 4, 16])
            akim = S.apow[:, 1, T0:T0 + 4, 0:16].rearrange("p t k -> p k t").unsqueeze(3).broadcast_to([128, 16, 4, 16])
            bre = S.Bbar[:, 0, T0:T0 + 4, :].unsqueeze(1).broadcast_to([128, 16, 4, 16])
            bim = S.Bbar[:, 1, T0:T0 + 4, :].unsqueeze(1).broadcast_to([128, 16, 4, 16])
            tA4 = tA[:, 0:1024].rearrange("p (k t c) -> p k t c", k=16, t=4)
            tB4 = tB[:, 0:1024].rearrange("p (k t c) -> p k t c", k=16, t=4)
            cmul(P, "dve", Wt[:, :, :, 0, :], Wt[:, :, :, 1, :], akre, akim, bre, bim, tA4, tB4,
                 ["apow", "Bbar"], ["Wt"], "tAB")
            if getattr(G, "dbgB", None) is not None and ct == G.dbg_ct:
                P.dma("sp", G.dbgB[:, 0:2176], E[:].rearrange("p k r n -> p (k r n)"), reads=["E"], writes=["dbgB"])
                P.dma("sp", G.dbgB[:, 2176:2176 + 2048], Wt[:].rearrange("p k t r c -> p (k t r c)"), reads=["Wt"], writes=["dbgB"])
                P.dma("sp", G.dbgB[:, 4224:4224 + 128], cch[:].rearrange("p r n -> p (r n)"), reads=["cch"], writes=["dbgB"])
            for k in range(17):
                xm = Xm[xi % 2]
                xk = f"Xm{xi % 2}"
                xi += 1
                src = E[:, k].rearrange("p r (q n) -> p (r q) n", n=16)
                tt(P, "dve", xm[:].rearrange("p a (g n) -> p a g n", n=16),
                   src.unsqueeze(2).broadcast_to([128, 8, 8, 16]),
                   bmask3.unsqueeze(1).broadcast_to([128, 8, 8, 16]), ALU.mult, ["E", "bmask"], [xk])
                pb = (k % 2) * 2
                for a in range(8):
                    bk = banks[pb + a // 4]
                    tr(P, bk[:, (a % 4) * 128:(a % 4 + 1) * 128], xm[:, a, :], G.ident[:], [xk, "ident"], [f"bb{pb + a // 4}"])
                for hb in range(2):
                    src_ps = banks[pb + hb][:].rearrange("p (q n) -> p q n", n=128)
                    if k == 0:
                        cp(P, "act", QW0[:, hb * 4:(hb + 1) * 4, :], src_ps, [f"bb{pb + hb}"], ["QW0"])
                    else:
                        cp(P, "act", QWs[:, :, k - 1, hb, :], src_ps, [f"bb{pb + hb}"], ["QWs"])
            T0 = ct * 4
            for ri in range(2):
                tt(P, "dve", Bm[:].rearrange("p (q r) (g c) -> p q r g c", r=2, c=16)[:, :, ri],
                   S.Bbar[:, ri, T0:T0 + 4, :].unsqueeze(2).broadcast_to([128, 4, 8, 16]),
                   bmask3.unsqueeze(1).broadcast_to([128, 4, 8, 16]), ALU.mult, ["Bbar", "bmask"], ["Bm"])
            for tau in range(16):
                cb = banks[tau % 2]
                cbk = f"bb{tau % 2}"
                n_mm = 0
                for nq in range(4):
                    for ri in range(2):
                        rhs = QW0[:, ri * 4 + nq, :] if tau == 0 else QWs[:, nq, tau - 1, ri, :]
                        mm(P, cb[:, 0:128], Bm[:, nq * 2 + ri, :], rhs, n_mm == 0, n_mm == 7, ["Bm", "QW0", "QWs"], [cbk])
                        n_mm += 1
                if tau == 0:
                    stt(P, "dve", KW[:, 0, :], G.ident[:], dcol[:, ct:ct + 1], cb[:, 0:128], ALU.mult, ALU.add,
                        [cbk, "ident", "vecs"], ["KW"])
                else:
                    cp(P, "dve", KW[:, tau, :], cb[:, 0:128], [cbk], ["KW"])
            for k in range(16):
                xm = Xm[xi % 2]
                xk = f"Xm{xi % 2}"
                xi += 1
                src = Wt[:, k].rearrange("p t r c -> p (t r) c")
                tt(P, "dve", xm[:].rearrange("p a (g n) -> p a g n", n=16),
                   src.unsqueeze(2).broadcast_to([128, 8, 8, 16]),
                   bmask3.unsqueeze(1).broadcast_to([128, 8, 8, 16]), ALU.mult, ["Wt", "bmask"], [xk])
                pb = 4 + (k % 2) * 2
                for a in range(8):
                    bk = banks[pb + a // 4]
                    tr(P, bk[:, (a % 4) * 128:(a % 4 + 1) * 128], xm[:, a, :], G.ident[:], [xk, "ident"], [f"bb{pb + a // 4}"])
                for hb in range(2):
                    src_ps = banks[pb + hb][:].rearrange("p (q r n) -> p q r n", r=2, n=128)
                    cp(P, "act", PWs[:, 2 * hb:2 * hb + 2, 15 - k, :, :], src_ps, [f"bb{pb + hb}"], ["PWs"])
            z0 = Z[0]
            u3 = u[:].rearrange("p (j s) -> p j s", s=T1)
            for nq in range(4):
                b = banks[nq]
                bk = f"bb{nq}"
                for ri in range(2):
                    for s in range(T1):
                        mm(P, b[:, ri * NJ:(ri + 1) * NJ], PWs[:, nq, s, ri, :], uS[:, s, :], s == 0, s == T1 - 1,
                           ["PWs", "uS"], [bk])
                cp(P, "act", z0[:, nq, :, 128:128 + NJ], b[:].rearrange("p (r j) -> p r j", r=2), [bk], [f"Z0_{nq}"])
            yg = ygT[0]
            ygk = "ygT0"
            yg3 = yg[:].rearrange("p (j s) -> p s j", s=T1)
            for s in range(T1):
                bnk = banks[s // 2]
                bk = f"bb{s // 2}"
                o = bnk[:, (s % 2) * NJ:(s % 2 + 1) * NJ]
                for tau in range(s + 1):
                    P.op("pe", lambda e, oo=o, l_=KW[:, tau, :], r_=uS[:, s - tau, :], st_=(tau == 0 and s % 2 == 0): e.matmul(oo, l_, r_, start=st_, stop=False, skip_group_check=True),
                         ["KW", "uS"], [bk])
            cur = 0
            T0 = ct * 4
            for lev in range(NLEV):
                d = 1 << lev
                zi, zo = Z[cur], Z[1 - cur]
                ki, ko = f"Z{cur}", f"Z{1 - cur}"
                for nq in range(4):
                    Ar = S.hsA[:, 0, T0 + nq, lev:lev + 1]
                    stt(P, "dve", H1[:, nq, :, :], zi[:, nq, :, 128 - d:128 - d + NJ], Ar, zi[:, nq, :, 128:128 + NJ], ALU.mult, ALU.add,
                        [f"{ki}_{nq}", "hsA"], [f"H1_{nq}"])
                for nq in range(4):
                    Ai = S.hsA[:, 1, T0 + nq, lev:lev + 1]
                    An = S.hsAn[:, T0 + nq, lev:lev + 1]
                    stt(P, "dve", zo[:, nq, 0, 128:128 + NJ], zi[:, nq, 1, 128 - d:128 - d + NJ], An, H1[:, nq, 0, :], ALU.mult, ALU.add,
                        [f"{ki}_{nq}", "hsA", f"H1_{nq}"], [f"{ko}_{nq}"])
                    stt(P, "dve", zo[:, nq, 1, 128:128 + NJ], zi[:, nq, 0, 128 - d:128 - d + NJ], Ai, H1[:, nq, 1, :], ALU.mult, ALU.add,
                        [f"{ki}_{nq}", "hsA", f"H1_{nq}"], [f"{ko}_{nq}"])
                cur = 1 - cur
            zf = Z[cur]
            kf = f"Z{cur}"
            cp(P, "act", Xbf[:, :, :, 1:NJ], zf[:, :, :, 128:128 + NJ - 1], [f"{kf}_{q}" for q in range(4)], ["Xbf"])
            for s in range(T1):
                bnk = banks[s // 2]
                bk = f"bb{s // 2}"
                o = bnk[:, (s % 2) * NJ:(s % 2 + 1) * NJ]
                n_mm = 0
                for nq in range(4):
                    for ri in range(2):
                        P.op("pe", lambda e, oo=o, l_=QWs[:, nq, s, ri, :], r_=Xbf[:, nq, ri, :], sp_=(n_mm == 7): e.matmul(oo, l_, r_, start=False, stop=sp_, skip_group_check=True),
                             ["QWs", "Xbf"], [bk])
                        n_mm += 1
                if s % 2 == 1:
                    act(P, yg3[:, s - 1:s + 1, :], bnk[:].rearrange("p (s j) -> p s j", s=2), AF.Gelu_apprx_tanh, [bk], [ygk])
            P.dma("act", yv[:, ct, :], yg[:], reads=[ygk], writes=["yg_scr"])
        P.emit(st)


NCMP = 255
QSCALE = 128 ** -0.5
TINY = 1e-30


def emit_cast_qg(P, G, jl):
    emit_cast_blocks(P, G.nsa_w_qg[jl], 0, 8, G.wbf_qg[jl], b0=0, cw=256)
    emit_cast_blocks(P, G.nsa_w_qg[jl], 2096, 24, G.wbf_qg[jl], b0=8, cw=256)
    emit_cast_blocks(P, G.nsa_w_qg[jl], 2048, 1, G.wbf_g[jl], b0=0, cw=48)
    emit_cast_blocks(P, G.nsa_w_o[jl], 0, 8, G.wbf_o[jl], cw=256)


def phase_kv(nc, G, x_src):
    with ExitStack() as st:
        P = Prog(nc)
        xc = sbt(nc, st, "xc", [128, KT, TC], F32)
        sq = sbt(nc, st, "sq", [128, KT, TC], BF16)
        hTs = [sbt(nc, st, f"hT{i}", [128, KT, TC], BF16) for i in range(2)]
        rstd = sbt(nc, st, "rstd", [128, TC], F32)
        wb = [sbt(nc, st, f"wb{i}", [128, KT, 512], BF16) for i in range(3)]
        og = [sbt(nc, st, f"og{i}", [128, 4, TC], BF16) for i in range(2)]
        banks = [pst(nc, st, f"kb{i}", [128, 512]) for i in range(8)]
        xv = x_src.rearrange("(k p) t -> p k t", p=128)
        ws = oi = bi = 0
        fm_slices = [0, 1, 2, 4]
        def norm_affine(cj):
            P.dma("sp", xc[:], xv[:, :, cj * TC:(cj + 1) * TC], writes=["xc"])
            emit_norm(P, G, xc, sq, rstd, banks[7], "kb7", {"xc": "xc", "sq": "sq", "rstd": "rstd"})
            emit_affine(P, hTs[cj % 2], xc, G.gkv, G.kvmodT[:, 0:16], "xc", f"hT{cj % 2}")

        norm_affine(0)
        for ci in range(NCH):
            t0 = ci * TC
            hT = hTs[ci % 2]
            hk = f"hT{ci % 2}"
            for sl in range(6):
                if sl == 3 and ci + 1 < NCH:
                    norm_affine(ci + 1)
                w = wb[ws % 3]
                wk = f"wb{ws % 3}"
                ws += 1
                P.dma("sp", w[:], wblk(G.wbf_kv, sl), reads=["wbf"], writes=[wk])
                o = og[oi % 2]
                ok = f"og{oi % 2}"
                oi += 1
                if sl in fm_slices:
                    for g in range(4):
                        b = banks[bi % 6]
                        bk = f"kb{bi % 6}"
                        bi += 1
                        for k in range(KT):
                            mm(P, b[:], w[:, k, g * 128:(g + 1) * 128], hT[:, k, :], k == 0, k == KT - 1, [wk, hk], [bk])
                        cp(P, "dve" if g % 2 else "act", o[:, g, :], b[:], [bk], [ok])
                    fi = fm_slices.index(sl)
                    P.dma("pool", G.kvT_scr[fi].rearrange("g p t -> p g t")[:, :, t0:t0 + TC], o[:], reads=[ok], writes=["kv_scr"])
                else:
                    vi = 0 if sl == 3 else 1
                    for tt_ in range(4):
                        b = banks[bi % 6]
                        bk = f"kb{bi % 6}"
                        bi += 1
                        for k in range(KT):
                            mm(P, b[:], hT[:, k, tt_ * 128:(tt_ + 1) * 128], w[:, k, :], k == 0, k == KT - 1, [wk, hk], [bk])
                        cp(P, "dve" if tt_ % 2 else "act", o[:, tt_, :], b[:], [bk], [ok])
                    for g in range(4):
                        P.dma("pool", G.vtok_scr[vi, g].rearrange("p (a d) -> p a d", d=128)[:, ci * 4:(ci + 1) * 4, :],
                              o[:, :, g * 128:(g + 1) * 128], reads=[ok], writes=["kv_scr"])
        P.emit(st)


def phase_cmp(nc, G):
    with ExitStack() as st:
        P = Prog(nc)
        src = [sbt(nc, st, f"csrc{i}", [128, L], BF16) for i in range(2)]
        w1 = [sbt(nc, st, f"cw1{i}", [128, 32, 128], BF16) for i in range(2)]
        w2 = [sbt(nc, st, f"cw2{i}", [128, 128], BF16) for i in range(2)]
        pe32 = sbt(nc, st, "cpe32", [128, 2, 32], F32)
        pe16 = sbt(nc, st, "cpe16", [128, 2, 32], BF16)
        b1 = sbt(nc, st, "cb1", [128, 2], F32)
        b2c = sbt(nc, st, "cb2c", [128, 2], F32)
        b2r = sbt(nc, st, "cb2r", [128, 128], F32)
        bias1 = sbt(nc, st, "cbias1", [128, 2], F32)
        hid = sbt(nc, st, "chid", [128, 256], BF16)
        banks = [pst(nc, st, f"cpb{i}", [128, 512]) for i in range(4)]
        P.dma("sp", pe32[:], G.cmp_peT, writes=["pe32"])
        cp(P, "dve", pe16[:], pe32[:], ["pe32"], ["pe16"])
        P.dma("sp", b1[:], G.cmp_b1T, writes=["b1"])
        P.dma("sp", b2c[:], G.cmp_b2T, writes=["b2c"])
        P.dma("sp", b2r[:], G.cmp_b2row, writes=["b2r"])
        memset(P, "dve", G.kcT[:], 0.0, ["kcT"])
        memset(P, "dve", G.vc[:], 0.0, ["vc"])
        memset(P, "dve", hid[:], 0.0, ["hid"])
        for i in range(2):
            P.dma("pool", w1[i][:], G.cmp_w1[i].rearrange("(j d) o -> d j o", d=128), writes=[f"w1{i}"])
            P.dma("pool", w2[i][:], G.cmp_w2[i], writes=[f"w2{i}"])
            for j in range(32):
                mm(P, banks[3][:, 0:1], w1[i][:, j, :], pe16[:, i, j:j + 1], j == 0, j == 31, [f"w1{i}", "pe16"], ["cpb3"])
            tt(P, "dve", bias1[:, i:i + 1], banks[3][:, 0:1], b1[:, i:i + 1], ALU.add, ["cpb3", "b1"], ["bias1"])
        n_it = 0
        for g in range(4):
            for i in range(2):
                s = src[n_it % 2]
                sk = f"csrc{n_it % 2}"
                b = banks[n_it % 2]
                bk = f"cpb{n_it % 2}"
                n_it += 1
                P.dma("sp", s[:], G.kvT_scr[i, g], reads=["kv_scr"], writes=[sk])
                for j in range(32):
                    rhs = s[:, j:j + 16 * (NCMP - 1) + 1:16]
                    mm(P, b[:, 0:NCMP], w1[i][:, j, :], rhs, j == 0, j == 31, [f"w1{i}", sk], [bk])
                act(P, hid[:, 0:NCMP], b[:, 0:NCMP], AF.Gelu_apprx_tanh, [bk, "bias1"], ["hid"], bias=bias1[:, i:i + 1], scale=1.0)
                if i == 0:
                    mm(P, banks[2][:, 0:NCMP], w2[0][:], hid[:, 0:NCMP], True, True, ["w20", "hid"], ["cpb2"])
                    act(P, G.kcT[:, g, 0:NCMP], banks[2][:, 0:NCMP], AF.Identity, ["cpb2", "b2c"], ["kcT"], bias=b2c[:, 0:1], scale=1.0)
                else:
                    for nt in range(2):
                        nn = 128 if nt == 0 else NCMP - 128
                        mm(P, banks[2][0:nn, nt * 128:(nt + 1) * 128], hid[:, nt * 128:nt * 128 + nn], w2[1][:], True, True,
                           ["w21", "hid"], ["cpb2"])
                        tt(P, "dve", G.vc[0:nn, g, nt, :], banks[2][0:nn, nt * 128:(nt + 1) * 128], b2r[0:nn, :], ALU.add,
                           ["cpb2", "b2r"], ["vc"])
        P.emit(st)


def phase_nsa(nc, G, layer, x_src):
    jl = layer - 2
    Wq = G.wbf_qg[jl]
    with ExitStack() as st:
        P = Prog(nc)
        big1 = sbt(nc, st, "big1", [128, KT * TC * 4], mybir.dt.uint8)
        xc = big1[:].bitcast(F32).rearrange("p (k t) -> p k t", t=TC)
        maskT = big1[:].bitcast(BF16).rearrange("p (k t) -> p k t", t=TC)
        big2 = sbt(nc, st, "big2", [128, KT, TC], BF16)
        hT = sbt(nc, st, "hT", [128, KT, TC], BF16)
        rstd = sbt(nc, st, "rstd", [128, TC], F32)
        wb = [sbt(nc, st, f"wb{i}", [128, KT, 256], BF16) for i in range(4)]
        wg = sbt(nc, st, "wg", [128, KT, 48], BF16)
        qT = sbt(nc, st, "qT", [128, 4, TC], BF16)
        zT = sbt(nc, st, "zT", [128, 12, TC], BF16)
        gT = sbt(nc, st, "gT", [48, TC], F32)
        grep_ = [sbt(nc, st, f"grep{i}", [128, TC], F32) for i in range(4)]
        ksT = sbt(nc, st, "ksT", [128, L], BF16)
        vs = sbt(nc, st, "vs", [128, 32, 128], BF16)
        kwT = sbt(nc, st, "kwT", [128, 1024], BF16)
        vw = sbt(nc, st, "vw", [128, 8, 128], BF16)
        Pe = [sbt(nc, st, f"Pe{i}", [128, TC], BF16) for i in range(5)]
        Pm = [sbt(nc, st, f"Pm{i}", [128, TC], BF16) for i in range(5)]
        pc = sbt(nc, st, "pc", [128, 2, TC], F32)
        ptmp = sbt(nc, st, "ptmp", [128, TC], F32)
        rinv = [sbt(nc, st, f"rinv{i}", [128, TC], F32) for i in range(2)]
        o1 = [sbt(nc, st, f"o1{i}", [128, TC], F32) for i in range(2)]
        oacc = sbt(nc, st, "oacc", [128, TC], F32)
        cmk = sbt(nc, st, "cmk", [128, 2, TC], BF16)
        cau = sbt(nc, st, "cau", [128, 4, TC], BF16)
        win = sbt(nc, st, "win", [128, 8, TC], BF16)
        Esel = sbt(nc, st, "Esel", [64, 32, 128], BF16)
        ov = sbt(nc, st, "ov", [128, 2, 64], F32)
        selA = sbt(nc, st, "selA", [128, 4, 64], F32)
        selB = sbt(nc, st, "selB", [128, 4, 64], F32)
        score = sbt(nc, st, "score", [128, 4, 64], F32)
        work = sbt(nc, st, "work", [128, 4, 64], F32)
        m8 = sbt(nc, st, "m8", [128, 4, 16], F32)
        self_ = sbt(nc, st, "self", [128, 4, 64], F32)
        selTb = sbt(nc, st, "selTb", [64, TC], BF16)
        tinyc = sbt(nc, st, "tinyc", [128, 1], F32)
        banks = [pst(nc, st, f"nb{i}", [128, 512]) for i in range(8)]
        memset(P, "dve", tinyc[:], TINY, ["tinyc"])
        xv = x_src.rearrange("(k p) t -> p k t", p=128)
        xo = G.xres.rearrange("(k p) t -> p k t", p=128)
        gate = G.modT[layer][:, 32:48]
        if layer == 2:
            emit_cast_qg(P, G, 1)
        P.dma("sp", cau[:], G.cauM.rearrange("k p t -> p k t"), writes=["cau"])
        P.dma("sp", win[:], G.winM.rearrange("k p t -> p k t"), writes=["win"])
        P.dma("sp", Esel[:], G.Esel_d, writes=["Esel"])
        P.dma("sp", ov[:], G.ov_d.rearrange("k p s -> p k s"), writes=["ov"])
        RS = RotState(banks)
        wseq = []
        for ci_ in range(NCH):
            for g_ in range(4):
                subs_ = [(2 * g_ + hf, "q", hf * 2) for hf in range(2)]
                for br_ in range(3):
                    subs_ += [(8 + (br_ * 4 + g_) * 2 + hf, "z", br_ * 4 + hf * 2) for hf in range(2)]
                wseq += [("qg", s_) for s_ in subs_]
            wseq += [("o", sb_) for sb_ in range(8)]
        wstate = {"issued": 0, "used": 0}

        def w_issue_upto(n):
            while wstate["issued"] < min(n, len(wseq)):
                i = wstate["issued"]
                kind_, info = wseq[i]
                src = wblk(Wq, info[0], 256) if kind_ == "qg" else wblk(G.wbf_o[jl], info, 256)
                P.dma("sp", wb[i % 4][:], src, reads=["wbf"], writes=[f"wb{i % 4}"])
                wstate["issued"] += 1

        def w_next():
            i = wstate["used"]
            w_issue_upto(i + 1)
            wstate["used"] += 1
            return wb[i % 4], f"wb{i % 4}", i

        w_issue_upto(4)
        for ci in range(NCH):
            t0 = ci * TC
            nkt = 4 * ci + 4
            sq = big2
            P.dma("sp", xc, xv[:, :, t0:t0 + TC], writes=["big1"])
            P.dma("sp", cmk[:], G.cmaskc[ci].rearrange("k p t -> p k t"), writes=["cmk"])
            P.dma("sp", selA[:], G.selA_d[:, ci * 4:(ci + 1) * 4, :], writes=["selA"])
            P.dma("sp", selB[:], G.selB_d[:, ci * 4:(ci + 1) * 4, :], writes=["selB"])
            act(P, sq[:], xc, AF.Square, ["big1"], ["big2"])
            for k in range(KT):
                mm(P, banks[7][:], G.ones_bf[:], sq[:, k, :], k == 0, k == KT - 1, ["ones_bf", "big2"], ["nb7"])
            act(P, rstd[:], banks[7][:], AF.Sqrt, ["nb7", "epsc"], ["rstd"], bias=G.epsc[:, 0:1], scale=1.0 / D)
            P.op("dve", lambda e, o=rstd[:], i=rstd[:]: e.reciprocal(out=o, in_=i), ["rstd"], ["rstd"])
            tt(P, "dve", xc, xc, rstd[:].unsqueeze(1).broadcast_to([128, KT, TC]), ALU.mult, ["big1", "rstd"], ["big1"])
            for k in range(KT):
                if k % 2 == 0:
                    act(P, hT[:, k, :], xc[:, k, :], AF.Identity, ["big1", "gsc", "modT"], ["hT"],
                        bias=G.modT[layer][:, k:k + 1], scale=G.gsc[layer][:, k:k + 1])
                else:
                    ts(P, "dve", hT[:, k, :], xc[:, k, :], G.gsc[layer][:, k:k + 1], ALU.mult, ["big1", "gsc", "modT"], ["hT"],
                       s2=G.modT[layer][:, k:k + 1], op1=ALU.add)
            P.dma("sp", wg[:], wblk(G.wbf_g[jl], 0, 48), reads=["wbf"], writes=["wg"])
            for k in range(KT):
                mm(P, banks[7][0:48, :], wg[:, k, :], hT[:, k, :], k == 0, k == KT - 1, ["wg", "hT"], ["nb7"])
            act(P, gT[:], banks[7][0:48, :], AF.Sigmoid, ["nb7"], ["gT"])
            P.dma("sp", G.g_scr[:, :], gT[:], reads=["gT"], writes=["g_scr"])
            oT = big2
            for g in range(4):
                P.dma("sp", ksT[:, 0:nkt * 128], G.kvT_scr[2, g][:, 0:nkt * 128], reads=["kv_scr"], writes=["ksT"])
                P.dma("sp", vs[:, 0:nkt, :], G.vtok_scr[0, g].rearrange("p (a d) -> p a d", d=128)[:, 0:nkt, :],
                      reads=["kv_scr"], writes=["vs"])
                kw0 = max(0, ci * 4 - 4)
                nkw = nkt - kw0
                P.dma("sp", kwT[:, 0:nkw * 128], G.kvT_scr[3, g][:, kw0 * 128:nkt * 128], reads=["kv_scr"], writes=["kwT"])
                P.dma("sp", vw[:, 0:nkw, :], G.vtok_scr[1, g].rearrange("p (a d) -> p a d", d=128)[:, kw0:nkt, :],
                      reads=["kv_scr"], writes=["vw"])
                subs = [(2 * g + hf, "q", hf * 2) for hf in range(2)]
                for br in range(3):
                    subs += [(8 + (br * 4 + g) * 2 + hf, "z", br * 4 + hf * 2) for hf in range(2)]
                for sidx, (blk, kind, j0) in enumerate(subs):
                    w, wk, wi = w_next()
                    for jj in range(2):
                        bsel = [7, 0, 1, 2][(sidx * 2 + jj) % 4]
                        b = banks[bsel]
                        bk = f"nb{bsel}"
                        for k in range(KT):
                            mm(P, b[:], w[:, k, jj * 128:(jj + 1) * 128], hT[:, k, :], k == 0, k == KT - 1, [wk, "hT"], [bk])
                        if kind == "q":
                            act(P, qT[:, j0 + jj, :], b[:], AF.Identity, [bk], ["qT"], scale=QSCALE)
                        else:
                            act(P, zT[:, j0 + jj, :], b[:], AF.Silu, [bk], ["zT"])
                    w_issue_upto(wi + 5)
                ntl = 1 if ci < 4 else 2

                def make_epilogue(kind, ob, okk, lb, lkk, gr0, gk0, pms, hh, h_glob):
                    def epi():
                        rv, rk = rinv[RS.ri % 2], f"rinv{RS.ri % 2}"
                        RS.ri += 1
                        act(P, rv[:], lb[:], AF.Ln, [lkk, "tinyc"], [rk], bias=tinyc[:, 0:1], scale=1.0)
                        act(P, rv[:], rv[:], AF.Exp, [rk], [rk], scale=-1.0)
                        if kind == "cmp":
                            for nt, (pm_, pmk) in enumerate(pms):
                                if hh == 0:
                                    tt(P, "pool", pc[:, nt, :], pm_[:], rv[:], ALU.mult, [pmk, rk], ["pc"])
                                else:
                                    tt(P, "pool", ptmp[:], pm_[:], rv[:], ALU.mult, [pmk, rk], ["ptmp"])
                                    tt(P, "pool", pc[:, nt, :], pc[:, nt, :], ptmp[:], ALU.add, ["pc", "ptmp"], ["pc"])
                            emit_combine(P, G, ob, okk, gr0, gk0, RS, o1, oacc, zT, rv, rk, 0, g, hh, first=True)
                            cp(P, "pool", oT[:, h_glob, :], oacc[:], ["oacc"], ["big2"])
                        elif kind == "win":
                            emit_combine(P, G, ob, okk, gr0, gk0, RS, o1, oacc, zT, rv, rk, 2, g, hh, first=True)
                            tt(P, "pool", oT[:, h_glob, :], oT[:, h_glob, :], oacc[:], ALU.add, ["big2", "oacc"], ["big2"])
                        else:
                            emit_combine(P, G, ob, okk, gr0, gk0, RS, o1, oacc, zT, rv, rk, 1, g, hh, first=True)
                            tt(P, "dve", oT[:, h_glob, :], oT[:, h_glob, :], oacc[:], ALU.add, ["big2", "oacc"], ["big2"])
                    return epi

                for hh in range(4):
                    h_glob = g * 4 + hh
                    tiles = [(G.kcT[:, g, nt * 128:(nt + 1) * 128], ["kcT"], cmk[:, nt, :], ["cmk"], G.vc[:, g, nt, :], ["vc"]) for nt in range(ntl)]
                    ob, okk, lb, lkk = RS.next_ol()
                    gr0, gk0 = emit_gate_dma(P, G, grep_, RS, 0, g, hh)
                    pms = attn_block(P, G, RS, banks, Pe, Pm, tiles, qT[:, hh, :], ob, okk, lb, lkk)
                    RS.pending = make_epilogue("cmp", ob, okk, lb, lkk, gr0, gk0, pms, hh, h_glob)

                def sel_chain(t4):
                    for nt in range(ntl):
                        mm(P, banks[7][:, t4 * 64:(t4 + 1) * 64], pc[:, nt, t4 * 128:(t4 + 1) * 128], ov[:, nt, :], nt == 0, nt == ntl - 1,
                           ["pc", "ov"], ["nb7"])
                    sc_, wk_, m8_, sf_ = score[:, t4, :], work[:, t4, :], m8[:, t4, :], self_[:, t4, :]
                    kk = [f"score{t4}", f"work{t4}", f"m8{t4}", f"self{t4}"]
                    tt(P, "dve", sc_, banks[7][:, t4 * 64:(t4 + 1) * 64], selA[:, t4, :], ALU.mult, ["nb7", "selA"], [kk[0]])
                    tt(P, "dve", sc_, sc_, selB[:, t4, :], ALU.add, [kk[0], "selB"], [kk[0]])
                    P.op("dve", lambda e, o=m8_[:, 0:8], i=sc_: e.max(out=o, in_=i), [kk[0]], [kk[2]])
                    P.op("dve", lambda e, o=wk_, a=m8_[:, 0:8], b=sc_: e.match_replace(out=o, in_to_replace=a, in_values=b, imm_value=-1e9),
                         [kk[0], kk[2]], [kk[1]])
                    P.op("dve", lambda e, o=m8_[:, 8:16], i=wk_: e.max(out=o, in_=i), [kk[1]], [kk[2]])
                    ts(P, "dve", wk_, sc_, m8_[:, 15:16], ALU.is_ge, [kk[0], kk[2]], [kk[1]])
                    stt(P, "dve", sf_, sc_, -500.0, wk_, ALU.is_gt, ALU.mult, [kk[0], kk[1]], [kk[3]])

                def sel_transpose(t4):
                    tr(P, banks[7][0:64, 256 + (t4 % 2) * 128:256 + (t4 % 2 + 1) * 128], self_[:, t4, :], G.ident[:], [f"self{t4}", "ident"], ["nb7"])
                    cp(P, "act", selTb[:, t4 * 128:(t4 + 1) * 128], banks[7][0:64, 256 + (t4 % 2) * 128:256 + (t4 % 2 + 1) * 128], ["nb7"], ["selTb"])

                for hh in range(4):
                    h_glob = g * 4 + hh
                    tiles = []
                    for a_ in range(nkw):
                        rel = (kw0 + a_) - (ci * 4 - 4)
                        tiles.append((kwT[:, a_ * 128:(a_ + 1) * 128], ["kwT"], win[:, rel, :], ["win"], vw[:, a_, :], ["vw"]))
                    ob, okk, lb, lkk = RS.next_ol()
                    gr0, gk0 = emit_gate_dma(P, G, grep_, RS, 2, g, hh)
                    attn_block(P, G, RS, banks, Pe, Pm, tiles, qT[:, hh, :], ob, okk, lb, lkk)
                    RS.pending = make_epilogue("win", ob, okk, lb, lkk, gr0, gk0, None, hh, h_glob)
                    if hh >= 1:
                        sel_transpose(hh - 1)
                    sel_chain(hh)
                sel_transpose(3)
                for kt in range(nkt):
                    sb_, sk = RS.next_s(banks)
                    mm(P, sb_[:], Esel[:, kt, :], selTb[:], True, True, ["Esel", "selTb"], [sk])
                    if kt >= 4 * ci:
                        tt(P, "dve", maskT[:, kt, :], sb_[:], cau[:, kt - 4 * ci, :], ALU.mult, [sk, "cau"], ["big1"])
                    else:
                        cp(P, "act" if kt % 2 else "dve", maskT[:, kt, :], sb_[:], [sk], ["big1"])
                for hh in range(4):
                    h_glob = g * 4 + hh
                    tiles = [(ksT[:, kt * 128:(kt + 1) * 128], ["ksT"], maskT[:, kt, :], ["big1"], vs[:, kt, :], ["vs"]) for kt in range(nkt)]
                    ob, okk, lb, lkk = RS.next_ol()
                    gr0, gk0 = emit_gate_dma(P, G, grep_, RS, 1, g, hh)
                    attn_block(P, G, RS, banks, Pe, Pm, tiles, qT[:, hh, :], ob, okk, lb, lkk)
                    RS.pending = make_epilogue("sel", ob, okk, lb, lkk, gr0, gk0, None, hh, h_glob)
                RS.flush()
            P.dma("sp", xc, xv[:, :, t0:t0 + TC], reads=["x_scr_prev"], writes=["big1"])
            for sb in range(8):
                w, wk, wi = w_next()
                for jj in range(2):
                    ct = sb * 2 + jj
                    bsel = [7, 0, 1, 2][ct % 4]
                    b = banks[bsel]
                    bk = f"nb{bsel}"
                    for k in range(KT):
                        mm(P, b[:], w[:, k, jj * 128:(jj + 1) * 128], oT[:, k, :], k == 0, k == KT - 1, [wk, "big2"], [bk])
                    stt(P, "dve", xc[:, ct, :], b[:], gate[:, ct:ct + 1], xc[:, ct, :], ALU.mult, ALU.add, [bk, "modT", "big1"], ["big1"])
                w_issue_upto(wi + 5)
            P.dma("pool", xo[:, :, t0:t0 + TC], xc, reads=["big1"], writes=["x_scr"])
        P.emit(st)


class RotState:
    def __init__(self, banks):
        self.si = 0
        self.pi = 0
        self.gi = 0
        self.ri = 0
        self.oi = 0
        self.banks = banks
        self.pending = None

    def flush(self):
        if self.pending is not None:
            f = self.pending
            self.pending = None
            f()

    def next_s(self, banks=None):
        i = [0, 1, 2, 7][self.si % 4]
        self.si += 1
        return self.banks[i], f"nb{i}"

    def next_p(self, Pe, Pm):
        i = self.pi % len(Pe)
        self.pi += 1
        return Pe[i], f"Pe{i}", Pm[i], f"Pm{i}", self.pi

    def next_ol(self):
        i = self.oi % 2
        self.oi += 1
        return self.banks[3 + i], f"nb{3 + i}", self.banks[5 + i], f"nb{5 + i}"


def attn_block(P, G, RS, banks, Pe, Pm, tiles, q_ap, ob, okk, lb, lkk, LA=3):
    n = len(tiles)
    slots = []
    for i in range(n + LA):
        if i < n:
            kT_ap, kk, mask_ap, mk, v_ap, vk = tiles[i]
            sb_, sk = RS.next_s()
            mm(P, sb_[:], kT_ap, q_ap, True, True, list(kk) + ["qT"], [sk])
            pe_, pek, pm_, pmk, cnt = RS.next_p(Pe, Pm)
            act(P, pe_[:], sb_[:], AF.Exp, [sk], [pek])
            tt(P, "dve", pm_[:], pe_[:], mask_ap, ALU.mult, [pek] + list(mk), [pmk])
            slots.append((pm_, pmk))
        if i == min(n, LA) - 1:
            RS.flush()
        j = i - LA
        if j >= 0:
            pm_, pmk = slots[j]
            v_ap, vk = tiles[j][4], tiles[j][5]
            mm(P, ob[:], v_ap, pm_[:], j == 0, j == n - 1, list(vk) + [pmk], [okk])
            mm(P, lb[:], G.ones_bf[:], pm_[:], j == 0, j == n - 1, ["ones_bf", pmk], [lkk])
    return slots


def emit_gate_dma(P, G, grep_, RS, br, g, hh):
    r = br * 16 + g * 4 + hh
    i = RS.gi % len(grep_)
    RS.gi += 1
    gr, gk = grep_[i], f"grep{i}"
    P.dma("pool", gr[:], G.g_scr[r].partition_broadcast(128), reads=["g_scr"], writes=[gk])
    return gr, gk


def emit_combine(P, G, ob, okk, gr, gk, RS, o1, oacc, zT, rv, rk, br, g, hh, first):
    o, ok = o1[RS.gi % 2], f"o1{RS.gi % 2}"
    tt(P, "dve", o[:], rv[:], gr[:], ALU.mult, [rk, gk], [ok])
    tt(P, "dve", o[:], ob[:], o[:], ALU.mult, [okk, ok], [ok])
    if first:
        tt(P, "pool", oacc[:], o[:], zT[:, br * 4 + hh, :], ALU.mult, [ok, "zT"], ["oacc"])
    else:
        tt(P, "pool", o[:], o[:], zT[:, br * 4 + hh, :], ALU.mult, [ok, "zT"], [ok])
        tt(P, "pool", oacc[:], oacc[:], o[:], ALU.add, ["oacc", ok], ["oacc"])


from concourse.bass_utils import run_bass_kernel_spmd
import ml_dtypes


def phase_final(nc, G, x_src):
    with ExitStack() as st:
        P = Prog(nc)
        xc = [sbt(nc, st, f"xc{i}", [128, KT, TC], F32) for i in range(2)]
        sq = sbt(nc, st, "sq", [128, KT, TC], BF16)
        rstd = sbt(nc, st, "rstd", [128, TC], F32)
        bank = pst(nc, st, "fb0", [128, 512])
        xv = x_src.rearrange("(k p) t -> p k t", p=128)
        yv = G.yT.rearrange("(k p) t -> p k t", p=128)
        gfin = G.vecs[:, 5, :]
        for ci in range(NCH):
            t0 = ci * TC
            x = xc[ci % 2]
            kx = f"xc{ci % 2}"
            P.dma("sp", x[:], xv[:, :, t0:t0 + TC], writes=[kx])
            emit_norm(P, G, x, sq, rstd, bank, "fb0", {"xc": kx, "sq": "sq", "rstd": "rstd"})
            for k in range(KT):
                if k % 2 == 0:
                    act(P, x[:, k, :], x[:, k, :], AF.Identity, [kx, "vecs"], [kx], scale=gfin[:, k:k + 1])
                else:
                    ts(P, "dve", x[:, k, :], x[:, k, :], gfin[:, k:k + 1], ALU.mult, [kx, "vecs"], [kx])
            P.dma("act", yv[:, :, t0:t0 + TC], x[:], reads=[kx], writes=["y_out"])
        P.emit(st)


def build(cfg):
    nc = bass.Bass("TRN2", target_bir_lowering=False)
    dbg = cfg.get("debug", ())

    def din(name, shape, dt=F32):
        return nc.dram_tensor(name, list(shape), dt, kind="ExternalInput").ap()

    def dscr(name, shape, dt):
        kind = "ExternalOutput" if name in dbg else "Internal"
        return nc.dram_tensor(name, list(shape), dt, kind=kind).ap()

    G = Ctx()
    G.xT = din("xT", [D, L])
    G.cT = din("cT", [128, 16])
    G.vecs_d = din("vecs", [128, 10, 16])
    G.ident_d = din("ident", [128, 128])
    G.bmask_d = din("bmask", [128, 128])
    G.mod_w = din("mod_w", [4, D, 3 * D])
    G.mod_b = din("mod_b", [4, 3 * D])
    G.kv_mod_w = din("kv_mod_w", [D, 2 * D])
    G.kv_mod_b = din("kv_mod_b", [1, 2 * D])
    G.ssm_w_in = din("ssm_w_in", [2, D, 2 * D])
    G.ssm_w_glu = din("ssm_w_glu", [2, D, D])
    G.ssm_w_out = din("ssm_w_out", [2, D, D])
    G.lamgn_d = din("lamgn", [2, 128, 3, 64])
    G.lamch_d = din("lamch", [2, 128, 3, 1024])
    G.bgn_d = din("bgn", [2, 128, 2, 64, 16])
    G.cch_d = din("cch", [2, 128, 2, 16, 64])
    G.w_kv = din("w_kv", [D, 3072])
    G.cmp_peT = din("cmp_peT", [128, 2, 32])
    G.cmp_b1T = din("cmp_b1T", [128, 2])
    G.cmp_b2T = din("cmp_b2T", [128, 2])
    G.cmp_b2row = din("cmp_b2row", [128, 128])
    G.cmp_w1 = din("cmp_w1", [2, 4096, 128])
    G.cmp_w2 = din("cmp_w2", [2, 128, 128])
    G.nsa_w_qg = din("nsa_w_qg", [2, D, 8240])
    G.nsa_w_o = din("nsa_w_o", [2, D, D])
    G.cmaskc = din("cmaskc", [8, 2, 128, 512], BF16)
    G.cauM = din("cauM", [4, 128, 512], BF16)
    G.winM = din("winM", [8, 128, 512], BF16)
    G.Esel_d = din("Esel", [64, 32, 128], BF16)
    G.ov_d = din("ov", [2, 128, 64])
    G.selA_d = din("selA", [128, 32, 64])
    G.selB_d = din("selB", [128, 32, 64])
    G.kvT_scr = dscr("kvT_scr", [4, 4, 128, L], BF16)
    G.vtok_scr = dscr("vtok_scr", [2, 4, 128, 32 * 128], BF16)
    G.g_scr = dscr("g_scr", [48, 512], F32)
    G.wbf_in = dscr("wbf_in", [2, 8, 128, 8192], BF16)
    G.wbf_glu = dscr("wbf_glu", [2, 4, 128, 8192], BF16)
    G.wbf_out = dscr("wbf_out", [2, 4, 128, 8192], BF16)
    G.wbf_kv = dscr("wbf_kv", [6, 128, 8192], BF16)
    G.wbf_qg = dscr("wbf_qg", [2, 32, 128, 4096], BF16)
    G.wbf_g = dscr("wbf_g", [2, 1, 128, 16 * 48], BF16)
    G.wbf_o = dscr("wbf_o", [2, 8, 128, 4096], BF16)
    G.uT_scr = dscr("uT_scr", [D, L], BF16)
    G.zT_scr = dscr("zT_scr", [D, L], BF16)
    G.ygT_scr = dscr("ygT_scr", [D, L], BF16)
    G.xres = dscr("xres", [D, L], F32)
    G.yT = nc.dram_tensor("yT", [D, L], F32, kind="ExternalOutput").ap()
    G.dbgB = None
    if "dbgB" in dbg:
        G.dbgB = nc.dram_tensor("dbgB", [128, 8192], F32, kind="ExternalOutput").ap()
        G.dbg_ct = cfg.get("dbg_ct", 1)

    with ExitStack() as gst:
        G.cact = sbt(nc, gst, "cact", [128, 16], F32)
        G.vecs = sbt(nc, gst, "vecs", [128, 10, 16], F32)
        G.ident = sbt(nc, gst, "ident", [128, 128], F32)
        G.bmask = sbt(nc, gst, "bmask", [128, 128], F32)
        G.ones_bf = sbt(nc, gst, "ones_bf", [128, 128], BF16)
        G.epsc = sbt(nc, gst, "epsc", [128, 1], F32)
        G.modT = [sbt(nc, gst, f"modT{l}", [128, 48], F32) for l in range(4)]
        G.kvmodT = sbt(nc, gst, "kvmodT", [128, 32], F32)
        G.gsc = [sbt(nc, gst, f"gsc{l}", [128, 16], F32) for l in range(4)]
        G.gkv = sbt(nc, gst, "gkv", [128, 16], F32)
        G.kcT = sbt(nc, gst, "kcT", [128, 4, 256], BF16)
        G.vc = sbt(nc, gst, "vc", [128, 4, 2, 128], BF16)
        phase_prologue(nc, G)
        if cfg.get("precast"):
            with ExitStack() as pst_:
                Pp = Prog(nc)
                if 1 in cfg["layers"] and 0 not in cfg["layers"]:
                    emit_cast_blocks(Pp, G.ssm_w_in[1], 0, 8, G.wbf_in[1])
                    emit_cast_blocks(Pp, G.ssm_w_glu[1], 0, 4, G.wbf_glu[1])
                    emit_cast_blocks(Pp, G.ssm_w_out[1], 0, 4, G.wbf_out[1])
                if 2 in cfg["layers"] and 1 not in cfg["layers"]:
                    emit_cast_blocks(Pp, G.w_kv, 0, 6, G.wbf_kv)
                    emit_cast_qg(Pp, G, 0)
                if 3 in cfg["layers"] and 2 not in cfg["layers"]:
                    emit_cast_blocks(Pp, G.w_kv, 0, 6, G.wbf_kv)
                    emit_cast_qg(Pp, G, 1)
                Pp.emit(pst_)
        x_cur = G.xT
        for layer in cfg["layers"]:
            if layer < 2:
                with ExitStack() as lst:
                    S = Ctx()
                    S.apow = sbt(nc, lst, "apow", [128, 2, 64, 17], F32)
                    S.hsA = sbt(nc, lst, "hsA", [128, 2, 64, 8], F32)
                    S.hsAn = sbt(nc, lst, "hsAn", [128, 64, 8], F32)
                    S.Bbar = sbt(nc, lst, "Bbar", [128, 2, 64, 16], F32)
                    S.a_ch = sbt(nc, lst, "a_ch", [128, 2, 1024], F32)
                    phase_s5_A(nc, G, layer, x_cur)
                    phase_s5_setup(nc, G, S, layer)
                    phase_s5_B(nc, G, S, layer)
                phase_s5_C(nc, G, layer, x_cur, G.xres)
                x_cur = G.xres
            else:
                if layer == 2 or not cfg.get("kv_done"):
                    cfg["kv_done"] = True
                    phase_kv(nc, G, x_cur)
                    phase_cmp(nc, G)
                if not cfg.get("kv_only"):
                    phase_nsa(nc, G, layer, x_cur)
                    x_cur = G.xres
        phase_final(nc, G, x_cur)
    return nc


def _vec16(v):
    return np.ascontiguousarray(np.asarray(v, np.float32).reshape(16, 128).T)


def host_common(inp):
    f = lambda a: np.ascontiguousarray(np.asarray(a, np.float32))
    m = {}
    vl = [inp["norm_g"][i] for i in range(4)] + [inp["kv_norm_g"], inp["final_norm_g"],
                                                 inp["ssm_d"][0], inp["ssm_d"][1], inp["ssm_b_glu"][0], inp["ssm_b_glu"][1]]
    m["vecs"] = np.ascontiguousarray(np.stack([_vec16(v) for v in vl], axis=1))
    m["ident"] = np.eye(128, dtype=np.float32)
    p = np.arange(128)
    m["bmask"] = (p[:, None] // 16 == p[None, :] // 16).astype(np.float32)
    for k in ["mod_w", "mod_b", "kv_mod_w", "ssm_w_in", "ssm_w_glu", "ssm_w_out"]:
        m[k] = f(inp[k])
    m["kv_mod_b"] = f(inp["kv_mod_b"]).reshape(1, -1)
    lamgn = np.zeros((2, 128, 3, 64), np.float32)
    lamch = np.zeros((2, 128, 3, 1024), np.float32)
    bgn = np.zeros((2, 128, 2, 64, 16), np.float32)
    cch = np.zeros((2, 128, 2, 16, 64), np.float32)
    for l in range(2):
        ls_full = np.broadcast_to(f(inp["ssm_log_step"][l])[:, None], (128, 64))
        for i, arr in enumerate([f(inp["ssm_lam_re"][l]), f(inp["ssm_lam_im"][l]), ls_full]):
            a5 = arr.reshape(16, 8, 4, 16)
            lamgn[l, :, i, :] = a5.transpose(1, 3, 0, 2).reshape(128, 64)
            a3 = arr.reshape(16, 8, 64).transpose(1, 0, 2)
            lamch[l, :, i, :] = np.broadcast_to(a3[:, None], (8, 16, 16, 64)).reshape(128, 1024)
        for i, arr in enumerate([f(inp["ssm_b_re"][l]), f(inp["ssm_b_im"][l])]):
            a6 = arr.reshape(16, 8, 4, 16, 16)
            bgn[l, :, i] = a6.transpose(1, 3, 0, 2, 4).reshape(128, 64, 16)
        for i, arr in enumerate([f(inp["ssm_c_re"][l]), f(inp["ssm_c_im"][l])]):
            a4 = arr.reshape(16, 8, 16, 64)
            cch[l, :, i] = a4.transpose(1, 2, 0, 3).reshape(128, 16, 64)
    m["lamgn"], m["lamch"], m["bgn"], m["cch"] = lamgn, lamch, bgn, cch
    for k in ["w_kv", "cmp_w1", "cmp_w2", "nsa_w_qg", "nsa_w_o"]:
        m[k] = f(inp[k])
    m["cmp_peT"] = np.ascontiguousarray(f(inp["cmp_pe"]).transpose(2, 0, 1))
    m["cmp_b1T"] = np.ascontiguousarray(f(inp["cmp_b1"]).T)
    m["cmp_b2T"] = np.ascontiguousarray(f(inp["cmp_b2"]).T)
    m["cmp_b2row"] = np.ascontiguousarray(np.broadcast_to(f(inp["cmp_b2"])[1][None, :], (128, 128)))
    bf = ml_dtypes.bfloat16
    kl = np.arange(128)[:, None]
    tl = np.arange(512)[None, :]
    cm = np.zeros((8, 2, 128, 512), np.float32)
    for ci in range(8):
        for nt in range(2):
            n = nt * 128 + kl
            cm[ci, nt] = ((16 * n + 31) <= (512 * ci + tl)) & (n < 255)
    m["cmaskc"] = cm.astype(bf)
    m["cauM"] = np.stack([((128 * kk + kl) <= tl) for kk in range(4)]).astype(np.float32).astype(bf)
    wm = []
    for rel in range(8):
        diff = tl - ((rel - 4) * 128 + kl)
        wm.append((diff >= 0) & (diff < 512))
    m["winM"] = np.stack(wm).astype(np.float32).astype(bf)
    s_ = np.arange(64)[:, None, None]
    kt_ = np.arange(32)[None, :, None]
    kk_ = np.arange(128)[None, None, :]
    m["Esel"] = (s_ == 2 * kt_ + kk_ // 64).astype(np.float32).astype(bf)
    c0 = np.arange(255)[:, None] * 16
    s0 = np.arange(64)[None, :] * 64
    ovm = (np.clip(np.minimum(c0 + 32, s0 + 64) - np.maximum(c0, s0), 0, None) / 16).astype(np.float32)
    ovp = np.zeros((256, 64), np.float32)
    ovp[:255] = ovm
    m["ov"] = np.ascontiguousarray(ovp.reshape(2, 128, 64))
    t = np.arange(4096)[:, None]
    blk = np.arange(64)[None, :]
    cur = t // 64
    valid = (blk <= cur)
    forced = ((blk == 0) | (blk == cur) | (blk == cur - 1))
    A = valid.astype(np.float32)
    B = (1000.0 * (forced & valid) - 1000.0 * (~valid)).astype(np.float32)
    m["selA"] = np.ascontiguousarray(A.reshape(32, 128, 64).transpose(1, 0, 2))
    m["selB"] = np.ascontiguousarray(B.reshape(32, 128, 64).transpose(1, 0, 2))
    return m


def host_percore(inp, b):
    return {"xT": np.ascontiguousarray(np.asarray(inp["x"][b], np.float32).T),
            "cT": _vec16(inp["c"][b])}


_NC_CACHE = {}


def kernel(**inputs):
    cfg = {"layers": [0, 1, 2, 3], "debug": ()}
    nc = build(cfg)
    common = host_common(inputs)
    in_maps = []
    for b in range(4):
        m = dict(common)
        m.update(host_percore(inputs, b))
        in_maps.append(m)
    res = run_bass_kernel_spmd(nc, in_maps, core_ids=list(range(4)))
    out = np.stack([np.ascontiguousarray(np.asarray(res.results[b]["yT"], np.float32).T) for b in range(4)], axis=0)
    return out
```
